# Optimizing a Trainium2 kernel written in Bass

```python
import jax
import jax.numpy as jnp
from jax import lax
import numpy as np

D_MODEL = 1024
BATCH = 16
SEQ = 2048
DEPTH = 1
DEC_BATCH = 32
DEC_SEQ = 1
PAST_LEN = 16384
PAGE_SIZE = 128

POOL_WINDOWS = (2, 4, 8, 16)
POOL_GROUP = D_MODEL // 8
POOL_WIDTH = POOL_GROUP * len(POOL_WINDOWS)
POOL_STATE = max(POOL_WINDOWS) - 1
HEAD_DIM = 64
HEADS_PER_GROUP = 4
ATTN_GROUPS = ((128, 1), (512, 4), (2048, 16))
N_ATTN_GROUPS = len(ATTN_GROUPS)
GROUP_WIDTH = HEADS_PER_GROUP * HEAD_DIM
QKV_WIDTH = N_ATTN_GROUPS * 3 * GROUP_WIDTH
IN_WIDTH = POOL_WIDTH + QKV_WIDTH + 2 * D_MODEL
FFN_DIM = ((8 * D_MODEL // 3 + 127) // 128) * 128
ROPE_THETA = 10000.0
RMS_EPS = 1e-6
Q_BLOCK = 128
NEG_INF = -1e30

kernel_name = "hybrid_pool_dilated_attn_macaron_step"


def rmsnorm(x, g):
    xf = x.astype(jnp.float32)
    y = xf * lax.rsqrt(jnp.mean(xf * xf, axis=-1, keepdims=True) + RMS_EPS)
    return (y * g.astype(jnp.float32)).astype(x.dtype)


def swiglu(x, w_gate, w_up, w_down):
    return (jax.nn.silu(x @ w_gate) * (x @ w_up)) @ w_down


def apply_rope(t, pos):
    half = HEAD_DIM // 2
    inv = ROPE_THETA ** (-jnp.arange(half, dtype=jnp.float32) * (2.0 / HEAD_DIM))
    ang = pos.astype(jnp.float32)[:, None] * inv[None, :]
    cos = jnp.cos(ang)[None, :, None, :]
    sin = jnp.sin(ang)[None, :, None, :]
    tf = t.astype(jnp.float32)
    t1, t2 = tf[..., :half], tf[..., half:]
    return jnp.concatenate([t1 * cos - t2 * sin, t1 * sin + t2 * cos], axis=-1).astype(t.dtype)


def pool_mix(u_ext, pos, w_grp, scale):
    n, le, c = u_ext.shape
    L = le - POOL_STATE
    uf = u_ext.astype(jnp.float32)
    cs = jnp.concatenate([jnp.zeros((n, 1, c), jnp.float32), jnp.cumsum(uf, axis=1)], axis=1)
    tok = uf[:, POOL_STATE:]
    end = cs[:, POOL_STATE + 1:]
    groups = []
    for gi, w in enumerate(POOL_WINDOWS):
        lo, hi = gi * POOL_GROUP, (gi + 1) * POOL_GROUP
        begin = cs[:, POOL_STATE + 1 - w: POOL_STATE + 1 - w + L, lo:hi]
        cnt = jnp.minimum(pos + 1, w).astype(jnp.float32)[None, :, None]
        groups.append((end[..., lo:hi] - begin) / cnt - tok[..., lo:hi])
    p = jnp.stack(groups, axis=2)
    p = jnp.einsum('nlgc,gcd->nlgd', p, w_grp.astype(jnp.float32)).reshape(n, L, POOL_WIDTH)
    return (p * scale.astype(jnp.float32)).astype(u_ext.dtype)


def masked_softmax(sc, mask):
    sc = jnp.where(mask, sc, NEG_INF)
    mx = jnp.max(sc, axis=-1, keepdims=True)
    e = jnp.exp(sc - mx)
    den = jnp.sum(e, axis=-1, keepdims=True)
    return e / den, (mx + jnp.log(den))[..., 0]


def dilated_band_attention(q, k, v, window, dil):
    b, s, h, dh = q.shape
    nk = window // dil
    m = -(-s // dil)
    blk = min(Q_BLOCK, m)
    nb = -(-m // blk)
    mp = nb * blk

    def to_res(t):
        t = jnp.pad(t, ((0, 0), (0, m * dil - s), (0, 0), (0, 0)))
        return t.reshape(b, m, dil, h, dh).swapaxes(1, 2).reshape(b * dil, m, h, dh)

    qr = jnp.pad(to_res(q), ((0, 0), (0, mp - m), (0, 0), (0, 0)))
    kr = jnp.pad(to_res(k), ((0, 0), (nk, mp - m), (0, 0), (0, 0)))
    vr = jnp.pad(to_res(v), ((0, 0), (nk, mp - m), (0, 0), (0, 0)))
    kidx = jnp.arange(nb)[:, None] * blk + jnp.arange(blk + nk)[None, :]
    qb = qr.reshape(b * dil, nb, blk, h, dh).astype(jnp.float32)
    kb = kr[:, kidx].astype(jnp.float32)
    vb = vr[:, kidx].astype(jnp.float32)
    sc = jnp.einsum('nbqhd,nbkhd->nbhqk', qb, kb) * (HEAD_DIM ** -0.5)
    dist = jnp.arange(blk)[:, None] + nk - jnp.arange(blk + nk)[None, :]
    mask = ((dist >= 0) & (dist <= nk))[None] & (kidx >= nk)[:, None, :]
    p, lse = masked_softmax(sc, mask[None, :, None])
    o = jnp.einsum('nbhqk,nbkhd->nbqhd', p, vb)
    lse = lse.swapaxes(2, 3)

    def from_res(t):
        t = t.reshape((b, dil, mp) + t.shape[3:])[:, :, :m]
        t = t.swapaxes(1, 2).reshape((b, m * dil) + t.shape[3:])
        return t[:, :s]

    return from_res(o), from_res(lse)


def dilated_cached_attention(q, k_ext, v_ext, window, dil):
    t = q.shape[1]
    n_past = k_ext.shape[1] - t
    nk = window // dil
    kidx = n_past + jnp.arange(t)[:, None] - dil * jnp.arange(nk + 1)[None, :]
    mask = kidx >= 0
    kidx = jnp.maximum(kidx, 0)
    kb = k_ext[:, kidx].astype(jnp.float32)
    vb = v_ext[:, kidx].astype(jnp.float32)
    sc = jnp.einsum('nthd,ntkhd->nthk', q.astype(jnp.float32), kb) * (HEAD_DIM ** -0.5)
    p, lse = masked_softmax(sc, mask[None, :, None, :])
    o = jnp.einsum('nthk,ntkhd->nthd', p, vb)
    return o, lse


def decoder_layer(x, pos, pool_prefix, kv_prefix, lp):
    x = x + 0.5 * swiglu(rmsnorm(x, lp['norm_ffn1']), lp['ffn1_w_gate'], lp['ffn1_w_up'], lp['ffn1_w_down'])
    h = rmsnorm(x, lp['norm_mix'])
    n, L, _ = h.shape
    z = h @ lp['w_in']
    u = z[..., :POOL_WIDTH]
    qkv = z[..., POOL_WIDTH:POOL_WIDTH + QKV_WIDTH].reshape(n, L, N_ATTN_GROUPS, 3, HEADS_PER_GROUP, HEAD_DIM)
    gates = jax.nn.sigmoid(z[..., POOL_WIDTH + QKV_WIDTH:].astype(jnp.float32)).reshape(n, L, 2, D_MODEL)

    u_ext = jnp.concatenate([pool_prefix.astype(u.dtype), u], axis=1)
    pool = pool_mix(u_ext, pos, lp['w_pool_grp'], lp['pool_scale'])
    new_pool = u_ext[:, -POOL_STATE:]

    outs, lses, new_kv = [], [], []
    for g, (window, dil) in enumerate(ATTN_GROUPS):
        q = apply_rope(qkv[:, :, g, 0], pos)
        k = apply_rope(qkv[:, :, g, 1], pos)
        v = qkv[:, :, g, 2]
        if kv_prefix is None:
            o, lse = dilated_band_attention(q, k, v, window, dil)
            k_ext, v_ext = k, v
        else:
            buf = kv_prefix[g].astype(k.dtype)
            k_ext = jnp.concatenate([buf[:, :, 0], k], axis=1)
            v_ext = jnp.concatenate([buf[:, :, 1], v], axis=1)
            o, lse = dilated_cached_attention(q, k_ext, v_ext, window, dil)
        keep = min(window, k_ext.shape[1])
        new_kv.append(jnp.stack([k_ext[:, -keep:], v_ext[:, -keep:]], axis=2))
        outs.append(o)
        lses.append(lse)
    wts = jax.nn.softmax(jnp.stack(lses, axis=0), axis=0)
    attn = jnp.einsum('gnlh,gnlhd->nlhd', wts, jnp.stack(outs, axis=0))

    pool_br = (pool @ lp['w_pool_br']).astype(jnp.float32)
    attn_br = (attn.reshape(n, L, GROUP_WIDTH).astype(x.dtype) @ lp['w_attn_br']).astype(jnp.float32)
    merged = (gates[:, :, 0] * pool_br + gates[:, :, 1] * attn_br).astype(x.dtype)
    x = x + merged @ lp['w_o']
    x = x + 0.5 * swiglu(rmsnorm(x, lp['norm_ffn2']), lp['ffn2_w_gate'], lp['ffn2_w_up'], lp['ffn2_w_down'])
    return x, new_pool, new_kv


def setup_inputs(seed: int = 0) -> dict:
    key = jax.random.key(seed)
    ks = iter(jax.random.split(key, 32))
    f32 = jnp.float32

    def nrm(shape, scale):
        return scale * jax.random.normal(next(ks), shape, f32)

    def gain(shape):
        return 1.0 + 0.02 * jax.random.normal(next(ks), shape, f32)

    kv_shape = lambda w: (DEPTH, DEC_BATCH, min(w, PAST_LEN), 2, HEADS_PER_GROUP, HEAD_DIM)
    return {
        'x_prompt': nrm((BATCH, SEQ, D_MODEL), 1.0),
        'x_sample': nrm((DEC_BATCH, DEC_SEQ, D_MODEL), 1.0),
        'state_pool': nrm((DEPTH, DEC_BATCH, POOL_STATE, POOL_WIDTH), 1.0),
        'cache_kv_w128': nrm(kv_shape(ATTN_GROUPS[0][0]), 1.0),
        'cache_kv_w512': nrm(kv_shape(ATTN_GROUPS[1][0]), 1.0),
        'cache_kv_w2048': nrm(kv_shape(ATTN_GROUPS[2][0]), 1.0),
        'norm_ffn1': gain((DEPTH, D_MODEL)),
        'ffn1_w_gate': nrm((DEPTH, D_MODEL, FFN_DIM), D_MODEL ** -0.5),
        'ffn1_w_up': nrm((DEPTH, D_MODEL, FFN_DIM), D_MODEL ** -0.5),
        'ffn1_w_down': nrm((DEPTH, FFN_DIM, D_MODEL), FFN_DIM ** -0.5),
        'norm_mix': gain((DEPTH, D_MODEL)),
        'w_in': nrm((DEPTH, D_MODEL, IN_WIDTH), D_MODEL ** -0.5),
        'w_pool_grp': nrm((DEPTH, len(POOL_WINDOWS), POOL_GROUP, POOL_GROUP), POOL_GROUP ** -0.5),
        'pool_scale': gain((DEPTH, POOL_WIDTH)),
        'w_pool_br': nrm((DEPTH, POOL_WIDTH, D_MODEL), POOL_WIDTH ** -0.5),
        'w_attn_br': nrm((DEPTH, GROUP_WIDTH, D_MODEL), GROUP_WIDTH ** -0.5),
        'w_o': nrm((DEPTH, D_MODEL, D_MODEL), D_MODEL ** -0.5),
        'norm_ffn2': gain((DEPTH, D_MODEL)),
        'ffn2_w_gate': nrm((DEPTH, D_MODEL, FFN_DIM), D_MODEL ** -0.5),
        'ffn2_w_up': nrm((DEPTH, D_MODEL, FFN_DIM), D_MODEL ** -0.5),
        'ffn2_w_down': nrm((DEPTH, FFN_DIM, D_MODEL), FFN_DIM ** -0.5),
        'norm_final': gain((D_MODEL,)),
    }


def reference(x_prompt, x_sample, state_pool, cache_kv_w128, cache_kv_w512, cache_kv_w2048,
              norm_ffn1, ffn1_w_gate, ffn1_w_up, ffn1_w_down, norm_mix, w_in, w_pool_grp, pool_scale,
              w_pool_br, w_attn_br, w_o, norm_ffn2, ffn2_w_gate, ffn2_w_up, ffn2_w_down, norm_final):
    pos_p = jnp.arange(x_prompt.shape[1], dtype=jnp.int32)
    pos_s = PAST_LEN + jnp.arange(x_sample.shape[1], dtype=jnp.int32)
    hp, hs = x_prompt, x_sample
    pool_p, pool_s = [], []
    kv_p = [[] for _ in ATTN_GROUPS]
    kv_s = [[] for _ in ATTN_GROUPS]
    for l in range(DEPTH):
        lp = {
            'norm_ffn1': norm_ffn1[l], 'ffn1_w_gate': ffn1_w_gate[l], 'ffn1_w_up': ffn1_w_up[l],
            'ffn1_w_down': ffn1_w_down[l], 'norm_mix': norm_mix[l], 'w_in': w_in[l],
            'w_pool_grp': w_pool_grp[l], 'pool_scale': pool_scale[l], 'w_pool_br': w_pool_br[l],
            'w_attn_br': w_attn_br[l], 'w_o': w_o[l], 'norm_ffn2': norm_ffn2[l],
            'ffn2_w_gate': ffn2_w_gate[l], 'ffn2_w_up': ffn2_w_up[l], 'ffn2_w_down': ffn2_w_down[l],
        }
        zeros_pool = jnp.zeros((hp.shape[0], POOL_STATE, POOL_WIDTH), hp.dtype)
        hp, new_pool_p, new_kv_p = decoder_layer(hp, pos_p, zeros_pool, None, lp)
        hs, new_pool_s, new_kv_s = decoder_layer(
            hs, pos_s, state_pool[l], (cache_kv_w128[l], cache_kv_w512[l], cache_kv_w2048[l]), lp)
        pool_p.append(new_pool_p)
        pool_s.append(new_pool_s)
        for g in range(N_ATTN_GROUPS):
            kv_p[g].append(new_kv_p[g])
            kv_s[g].append(new_kv_s[g])
    y_prompt = rmsnorm(hp, norm_final)
    y_sample = rmsnorm(hs, norm_final)
    return (y_prompt, y_sample,
            jnp.stack(pool_p), jnp.stack(kv_p[0]), jnp.stack(kv_p[1]), jnp.stack(kv_p[2]),
            jnp.stack(pool_s), jnp.stack(kv_s[0]), jnp.stack(kv_s[1]), jnp.stack(kv_s[2]))
```

```python
import contextlib
import numpy as np
import concourse.bass as bass
import concourse.mybir as mybir
from concourse.bass_utils import run_bass_kernel_spmd

F32 = mybir.dt.float32
BF16 = mybir.dt.bfloat16
AF = mybir.ActivationFunctionType
ALU = mybir.AluOpType

D = 1024
FF = 2816
NJ = FF // 128
T = 512
SEQ = 2048
NSEQ = 2
NTILE = SEQ // T
NS = 4
PAST = 16384
EPS = 1e-6
GROUPS = ((128, 1), (512, 4), (2048, 16))
OFF_U = 0
OFF_Q = 512
OFF_K = 512 + 768
OFF_V = 512 + 1536
OFF_GP = OFF_V + 768
OFF_GA = OFF_GP + 1024
CFG = {"nseq": NSEQ, "ntile": NTILE, "sample": True}


class StopBuild(Exception):
    pass


class Res:
    __slots__ = ("w", "r", "name")

    def __init__(self, name=""):
        self.w = None
        self.r = {}
        self.name = name


class DmaSem:
    def __init__(self, K, name):
        self.sem = K.es.enter_context(K.nc.semaphore(name))
        self.total = 0


class EngW:
    def __init__(self, K, eng, name, is_pe=False):
        self.K = K
        self.eng = eng
        self.name = name
        self.sem = K.es.enter_context(K.nc.semaphore("sem_" + name))
        self.cnt = 0
        self.known = {}
        self.is_pe = is_pe

    def wait_tok(self, tok):
        sem, val = tok
        if sem is self.sem:
            if self.is_pe:
                return
            assert val <= self.cnt, (self.name, val, self.cnt)
        if self.known.get(id(sem), 0) >= val:
            return
        self.eng.wait_ge(sem, val)
        self.known[id(sem)] = val

    def wait_deps(self, reads, writes):
        for r in reads:
            if r.w is not None:
                self.wait_tok(r.w)
        for w in writes:
            if w.w is not None:
                self.wait_tok(w.w)
            for tok in w.r.values():
                self.wait_tok(tok)

    @staticmethod
    def commit(tok, reads, writes):
        for r in reads:
            old = r.r.get(id(tok[0]))
            if old is None or old[1] < tok[1]:
                r.r[id(tok[0])] = tok
        for w in writes:
            w.w = tok
            w.r = {}

    def op(self, fn, reads=(), writes=(), inc=True):
        self.nops = getattr(self, "nops", 0) + 1
        self.wait_deps(reads, writes)
        ins = fn()
        if inc:
            ins.then_inc(self.sem, 1)
            self.cnt += 1
            tok = (self.sem, self.cnt)
        else:
            tok = (self.sem, self.cnt + 1)
        self.commit(tok, reads, writes)
        return tok

    def dma(self, out, in_, dsem, reads=(), writes=(), **kw):
        self.wait_deps(reads, writes)
        self.eng.dma_start(out=out, in_=in_, **kw).then_inc(dsem.sem, 16)
        dsem.total += 16
        tok = (dsem.sem, dsem.total)
        self.commit(tok, reads, writes)
        return tok


class Buf:
    def __init__(self, K, name, shape, dtype, nres=1):
        K.uid = getattr(K, "uid", 0) + 1
        self.t = K.es_cur.enter_context(K.nc.sbuf_tensor("sb_%s_%d" % (name, K.uid), shape, dtype))
        self.r = [Res(name + str(i)) for i in range(nres)]

    def __getitem__(self, idx):
        return self.t[idx]


class Seg:
    def __init__(self, n, xT, hT, rstd, rope):
        self.n, self.xT, self.hT, self.rstd, self.rope = n, xT, hT, rstd, rope


class Ring:
    def __init__(self, K, name, shape, dtype, n, sems=None):
        self.slots = [Buf(K, "%s%d" % (name, i), shape, dtype) for i in range(n)]
        self.sems = sems if sems is not None else [DmaSem(K, "ds_%s%d" % (name, i)) for i in range(n)]
        self.n = n
        self.i = 0

    def next(self):
        s = self.i % self.n
        self.i += 1
        return self.slots[s], self.sems[s]


class Kern:
    def __init__(self):
        self.nc = bass.Bass("TRN2", target_bir_lowering=False)
        self.es = contextlib.ExitStack()
        self.es_cur = self.es

    def dram_in(self, name, shape):
        return self.nc.dram_tensor(name, list(shape), F32, kind="ExternalInput").ap()

    def dram_out(self, name, shape):
        return self.nc.dram_tensor(name, list(shape), F32, kind="ExternalOutput").ap()

    def setup(self):
        nc = self.nc
        self.pe = EngW(self, nc.tensor, "pe", is_pe=True)
        self.act = EngW(self, nc.scalar, "act")
        self.dve = EngW(self, nc.vector, "dve")
        self.pool = EngW(self, nc.gpsimd, "pool")
        self.sp = EngW(self, nc.sync, "sp")
        self.ps = [self.es.enter_context(nc.psum_tensor("ps%d" % i, [128, 512], F32)) for i in range(8)]
        self.psr = [Res("ps%d" % i) for i in range(8)]
        self.ps_free = list(range(8))
        self.ps_i = 0
        self.all_dsems = []
        self.arena_dsems = []
        self.evac_i = 0

    def dsem(self, name):
        d = DmaSem(self, name)
        self.all_dsems.append(d)
        return d

    def bank(self):
        b = self.ps_free[self.ps_i % len(self.ps_free)]
        self.ps_i += 1
        return b

    def mmg(self, bank, items, reads, first_start=True, stop_last=True, write=True):
        n = len(items)
        for i, (o, l, r, kw) in enumerate(items):
            st = kw.pop("start", (i == 0) and first_start)
            sp_ = kw.pop("stop", (i == n - 1) and stop_last)
            last = (i == n - 1)
            rd_i = kw.pop("rd", None)
            if kw.pop("is_transpose", False):
                fn = lambda o=o, l=l, r=r: self.nc.tensor.transpose(o, l, r)
            else:
                fn = lambda o=o, l=l, r=r, st=st, sp_=sp_, kw=kw: self.nc.tensor.matmul(o, l, r, start=st, stop=sp_, **kw)
            rds = (list(reads) if i == 0 else []) + (list(rd_i) if rd_i else [])
            self.pe.op(fn, reads=rds, writes=[self.psr[bank]] if (i == 0 and write) else (), inc=last)
        tok = (self.pe.sem, self.pe.cnt)
        EngW.commit(tok, reads, [self.psr[bank]])
        return tok

    def evac_eng(self):
        self.evac_i += 1
        return self.act if (self.evac_i % 2) else self.dve

    def copy(self, eng, out, in_, reads, writes):
        if eng is self.act:
            return eng.op(lambda: self.nc.scalar.copy(out=out, in_=in_), reads, writes)
        return eng.op(lambda: self.nc.vector.tensor_copy(out=out, in_=in_), reads, writes)

    def barrier_tokens(self):
        toks = []
        for e in (self.pe, self.act, self.dve, self.pool):
            if e.cnt > 0:
                toks.append((e.sem, e.cnt))
        for d in self.arena_dsems:
            if d.total > 0:
                toks.append((d.sem, d.total))
        return toks

    def barrier(self):
        toks = self.barrier_tokens()
        for e in (self.pe, self.act, self.dve):
            for tk in toks:
                if tk[0] is not e.sem:
                    e.wait_tok(tk)
        self.arena_toks = toks


def build(cfg=CFG):
    K = Kern()
    nc = K.nc
    nseq, ntile, do_sample = cfg["nseq"], cfg["ntile"], cfg["sample"]
    x_d = K.dram_in("x", [NSEQ, SEQ, D])
    xs_d = K.dram_in("xs", [NS, D])
    spool_d = K.dram_in("spool", [NS, 15, 512])
    cache_d = [K.dram_in("c%d" % w, [NS, w, 2, 256]) for (w, _) in GROUPS]
    gains_d = K.dram_in("gains", [4, D])
    pscale_d = K.dram_in("pscale", [512])
    wg_d = [K.dram_in("wg%d" % i, [D, FF]) for i in (1, 2)]
    wu_d = [K.dram_in("wu%d" % i, [D, FF]) for i in (1, 2)]
    wd_d = [K.dram_in("wd%d" % i, [FF, D]) for i in (1, 2)]
    win_d = K.dram_in("win", [D, 4864])
    wgrp_d = K.dram_in("wgrp", [4, 128, 128])
    wpb_d = K.dram_in("wpb", [512, D])
    wab_d = K.dram_in("wab", [256, D])
    wo_d = K.dram_in("wo", [D, D])
    ident_d = K.dram_in("ident", [128, 128])
    mask01_d = K.dram_in("mask01", [128, 2, 4, 128])
    mask2_d = K.dram_in("mask2", [128, 16, 32])
    rope_d = K.dram_in("rope", [5, 2, 128, 512])
    rcnt_d = K.dram_in("rcnt", [128, 4, 16])

    y_d = K.dram_out("y", [NSEQ, SEQ, D])
    ys_d = K.dram_out("ys", [NS, D])
    poolp_d = K.dram_out("poolp", [NSEQ, 15, 512])
    kvp_d = [K.dram_out("kvp%d" % w, [NSEQ, min(w, SEQ), 2, 256]) for (w, _) in GROUPS]
    pools_d = K.dram_out("pools", [NS, 15, 512])
    kvs_d = [K.dram_out("kvs%d" % w, [NS, w, 2, 256]) for (w, _) in GROUPS]

    with K.es:
        K.setup()
        pe, act, dve, pool, sp = K.pe, K.act, K.dve, K.pool, K.sp
        ps, psr = K.ps, K.psr

        ident = Buf(K, "ident", [128, 128], F32)
        mask01 = Buf(K, "mask01", [128, 2, 4, 128], BF16)
        mask2 = Buf(K, "mask2", [128, 16, 32], BF16)
        rcnt = Buf(K, "rcnt", [128, 4, 16], F32)
        gains = Buf(K, "gains", [128, 4, 8], F32)
        pscale = Buf(K, "pscale", [128, 4], F32)
        wgrp = Buf(K, "wgrp", [128, 4, 128], BF16)
        ones = Buf(K, "ones", [128, 128], BF16)
        dummy = Buf(K, "dummy", [128, 4], F32)
        meanm = Buf(K, "meanm", [128, 128], BF16)
        zeros = Buf(K, "zeros", [1, 512], BF16)
        xin = Buf(K, "xin", [128, 4, D], F32)
        xT = Buf(K, "xT", [128, 8, T], F32, nres=8)
        hT = Buf(K, "hT", [128, 8, T], BF16, nres=8)
        rstd = Buf(K, "rstd", [128, T], F32)
        rope = Buf(K, "rope", [128, 2, T], F32)
        ucarry = Buf(K, "ucarry", [128, 4, 15], F32)
        kT0 = Buf(K, "kT0", [128, 2, 5 * 128], BF16)
        kT1 = Buf(K, "kT1", [128, 2, 2 * 512], BF16)
        kT2 = Buf(K, "kT2", [128, 2, 16 * 128], BF16)
        V0 = Buf(K, "V0", [128, 5, 256], BF16)
        V1 = Buf(K, "V1", [128, 8, 256], BF16)
        V2 = Buf(K, "V2", [128, 16, 256], BF16)
        xTs = Buf(K, "xTs", [128, 8, NS], F32, nres=8)
        hTs = Buf(K, "hTs", [128, 8, NS], BF16, nres=8)
        rstds = Buf(K, "rstds", [128, NS], F32)
        rope_s = Buf(K, "rope_s", [128, 2, NS], F32)
        SP = Seg(T, xT, hT, rstd, rope)
        SS = Seg(NS, xTs, hTs, rstds, rope_s)
        w8 = Ring(K, "w8", [128, 8, 256], BF16, 7)
        w8_sem3 = [[K.dsem("ds_w8%s%d" % (k, i)) for k in "pwh"] for i in range(7)]
        wd_sem3 = [[K.dsem("ds_wd%s%d" % (k, i)) for k in "pwh"] for i in range(3)]
        wv_sem3 = [K.dsem("ds_wv%s" % k) for k in "pwh"]
        wd_sems = [wd_sem3[i][0] for i in range(3)]
        st_sems = [{k: K.dsem("ds_%s%d" % (k, i)) for k in ("kv", "kv2", "kt", "pt", "y", "ys")} for i in range(2)]
        ds_sp = K.dsem("ds_sp")
        K.arena_dsems = [d for tr in wd_sem3 for d in tr] + wv_sem3 + [st_sems[i][k] for i in range(2) for k in ("kv", "kv2", "kt", "pt")]
        K.pass_idx = 0
        wscr8 = nc.dram_tensor("wscr8", [72, 128, 2048], BF16, kind="Internal").ap()
        wscrd = nc.dram_tensor("wscrd", [8, 128, NJ * 256], BF16, kind="Internal").ap()
        wscrv = nc.dram_tensor("wscrv", [128, 8 * 768], BF16, kind="Internal").ap()
        scr8_res = [Res("scr8_%d" % i) for i in range(72)]
        scrd_res = [Res("scrd_%d" % i) for i in range(8)]
        scrv_res = Res("scrv")

        def st_dma(kind, out, in_, reads):
            if K.pass_idx == 0:
                return sp.dma(out, in_, st_sems[0][kind], reads=reads)
            return pool.dma(out, in_, st_sems[1][kind], reads=reads)

        def wload(first_pass, slot, slot_ap, src_f32, scr_ap, scr_res, sem3, arena_bound):
            if first_pass:
                if arena_bound:
                    for tk in K.arena_toks:
                        pool.wait_tok(tk)
                pool.dma(slot_ap, src_f32, sem3[0], writes=slot.r)
                sp.dma(scr_ap, slot_ap, sem3[1], reads=slot.r, writes=[scr_res])
            else:
                if arena_bound:
                    for tk in K.arena_toks:
                        sp.wait_tok(tk)
                sp.dma(slot_ap, scr_ap, sem3[2], reads=[scr_res], writes=slot.r)
        ds_const = K.dsem("ds_const")
        ds_constp = K.dsem("ds_constp")
        ds_x = K.dsem("ds_x")
        ds_y = K.dsem("ds_y")
        ds_rope = K.dsem("ds_rope")
        ds_misc = K.dsem("ds_misc")
        K.arena_toks = []

        sp.dma(ident[:], ident_d[:, :], ds_const, writes=ident.r)
        sp.dma(rcnt[:], rcnt_d[:, :, :], ds_const, writes=rcnt.r)
        sp.dma(pscale[:], pscale_d.rearrange("(c p) -> p c", p=128), ds_const, writes=pscale.r, allow_slow_non_contiguous=True)
        for i in range(4):
            sp.dma(gains[:, i, :], gains_d[i, :].rearrange("(c p) -> p c", p=128), ds_const, writes=gains.r,
                   allow_slow_non_contiguous=True)
        pool.dma(mask01[:], mask01_d[:, :, :, :], ds_constp, writes=mask01.r)
        pool.dma(mask2[:], mask2_d[:, :, :], ds_constp, writes=mask2.r)
        pool.dma(wgrp[:], wgrp_d.rearrange("g c d -> c g d"), ds_constp, writes=wgrp.r)
        for bufc in (ident, rcnt, pscale, gains):
            bufc.r[0].w = (ds_const.sem, ds_const.total)
        for bufc in (mask01, mask2, wgrp):
            bufc.r[0].w = (ds_constp.sem, ds_constp.total)
        dve.op(lambda: nc.vector.memset(ones[:], 1.0), writes=ones.r)
        dve.op(lambda: nc.vector.memset(dummy[:], 1.0), writes=dummy.r)
        act.op(lambda: nc.scalar.copy(out=dummy[:, 3:4], in_=dummy[:, 2:3]), reads=dummy.r)
        dve.op(lambda: nc.vector.memset(meanm[:], 1.0 / D), writes=meanm.r)
        dve.op(lambda: nc.vector.memset(zeros[:], 0.0), writes=zeros.r)

        w8_plan = []
        w8_state = {"issued": 0, "loaded": []}

        def w8_src(w_ap, c0, ncols, krows=D):
            kc = krows // 128
            return (w_ap.rearrange("(kc p) n -> p kc n", p=128)[:, :, c0:c0 + ncols], kc, ncols)

        def w8_issue_upto(n):
            while w8_state["issued"] < min(n, len(w8_plan)):
                ii = w8_state["issued"]
                src, kc, ncols = w8_plan[ii]
                pi_ = ii % 72
                si = w8.i % w8.n
                slot, _ = w8.next()
                wload(ii < 72, slot, slot[:, 0:kc, 0:ncols], src,
                      wscr8[pi_, :, 0:kc * ncols].rearrange("p (k n) -> p k n", n=ncols), scr8_res[pi_], w8_sem3[si], False)
                w8_state["loaded"].append(slot)
                w8_state["issued"] += 1

        w8_cons = {"i": 0}

        def w8_get():
            i = w8_cons["i"]
            w8_issue_upto(i + 1)
            slot = w8_state["loaded"][i]
            w8_cons["i"] += 1
            return slot

        def w8_done():
            w8_issue_upto(w8_cons["i"] + w8.n - 1)

        def plan_pass():
            for f in (0,):
                for jp in range(NJ // 2):
                    w8_plan.append(w8_src(wg_d[0], jp * 256, 256))
                    w8_plan.append(w8_src(wu_d[0], jp * 256, 256))
            for c0 in range(0, 512 + 1536, 256):
                w8_plan.append(w8_src(win_d, c0, 256))
            for mp in range(4):
                w8_plan.append(w8_src(win_d, OFF_GP + mp * 256, 256))
                w8_plan.append(w8_src(win_d, OFF_GA + mp * 256, 256))
                w8_plan.append(w8_src(wpb_d, mp * 256, 256, krows=512))
                w8_plan.append(w8_src(wab_d, mp * 256, 256, krows=256))
            for mp in range(4):
                w8_plan.append(w8_src(wo_d, mp * 256, 256))
            for jp in range(NJ // 2):
                w8_plan.append(w8_src(wg_d[1], jp * 256, 256))
                w8_plan.append(w8_src(wu_d[1], jp * 256, 256))

        n_pass = max(nseq * ntile, 1 if do_sample else 0)
        K.n_pass = n_pass
        for _ in range(n_pass):
            plan_pass()

        def norm(gi, seg, out_f32=None, next_func=None):
            n, xT_, hT_, rstd_ = seg.n, seg.xT, seg.hT, seg.rstd
            act.op(lambda: nc.scalar.activation(out=dummy[:, 0:1], in_=dummy[:, 2:3], func=AF.Ln))
            for c in range(8):
                if c % 2 == 0:
                    act.op(lambda c=c: nc.scalar.activation(out=hT_[:, c, 0:n], in_=xT_[:, c, 0:n], func=AF.Square),
                           reads=[xT_.r[c]], writes=[hT_.r[c]])
                else:
                    dve.op(lambda c=c: nc.vector.tensor_tensor(out=hT_[:, c, 0:n], in0=xT_[:, c, 0:n], in1=xT_[:, c, 0:n], op=ALU.mult),
                           reads=[xT_.r[c]], writes=[hT_.r[c]])
            b = K.bank()
            K.mmg(b, [(ps[b][:, 0:n], meanm[:], hT_[:, c, 0:n], {"rd": [hT_.r[c]]}) for c in range(8)], reads=meanm.r)
            act.op(lambda: nc.scalar.activation(out=rstd_[:, 0:n], in_=ps[b][:, 0:n], func=AF.Ln, bias=EPS, scale=1.0),
                   reads=[psr[b]], writes=rstd_.r)
            act.op(lambda: nc.scalar.activation(out=rstd_[:, 0:n], in_=rstd_[:, 0:n], func=AF.Exp, scale=-0.5), reads=rstd_.r,
                   writes=rstd_.r)
            if next_func is not None:
                act.op(lambda: nc.scalar.activation(out=dummy[:, 1:2], in_=dummy[:, 2:3], func=next_func))
            for c in range(8):
                if out_f32 is None:
                    dve.op(lambda c=c: nc.vector.scalar_tensor_tensor(
                        out=hT_[:, c, 0:n], in0=xT_[:, c, 0:n], scalar=gains[:, gi, c:c + 1], in1=rstd_[:, 0:n],
                        op0=ALU.mult, op1=ALU.mult), reads=[xT_.r[c]] + rstd_.r + gains.r, writes=[hT_.r[c]])
                else:
                    dve.op(lambda c=c: nc.vector.scalar_tensor_tensor(
                        out=xT_[:, c, 0:n], in0=xT_[:, c, 0:n], scalar=gains[:, gi, c:c + 1], in1=rstd_[:, 0:n],
                        op0=ALU.mult, op1=ALU.mult), reads=[xT_.r[c]] + rstd_.r + gains.r + hT_.r, writes=[xT_.r[c]])

        def ffn(f, segs):
            wdr = K.wdr
            wd_slots = []

            def wd_issue(mp):
                si = wdr.i % wdr.n
                slot, _ = wdr.next()
                wload(K.pass_idx == 0, slot, slot[:], wd_d[f].rearrange("(j p) n -> p j n", p=128)[:, :, mp * 256:(mp + 1) * 256],
                      wscrd[4 * f + mp, :, :].rearrange("p (j n) -> p j n", n=256), scrd_res[4 * f + mp], wd_sem3[si], True)
                wd_slots.append(slot)

            for mp in range(3):
                wd_issue(mp)
            for jp in range(NJ // 2):
                gs = w8_get()
                us = w8_get()
                for jj in range(2):
                    j = 2 * jp + jj
                    for seg in segs:
                        n, hT_, actT, stmp = seg.n, seg.hT, seg.actT, seg.stmp
                        bg = K.bank()
                        K.mmg(bg, [(ps[bg][:, 0:n], gs[:, kc, jj * 128:(jj + 1) * 128], hT_[:, kc, 0:n], {"rd": [hT_.r[kc]]}) for kc in range(8)],
                              reads=gs.r)
                        bu = K.bank()
                        K.mmg(bu, [(ps[bu][:, 0:n], us[:, kc, jj * 128:(jj + 1) * 128], hT_[:, kc, 0:n], {"rd": [hT_.r[kc]]}) for kc in range(8)],
                              reads=us.r)
                        st = stmp[j % 2]
                        act.op(lambda st=st, bg=bg, n=n: nc.scalar.activation(out=st[:, 0:n], in_=ps[bg][:, 0:n], func=AF.Silu),
                               reads=[psr[bg]], writes=st.r)
                        dve.op(lambda st=st, bu=bu, j=j, n=n, actT=actT: nc.vector.tensor_tensor(
                            out=actT[:, j, 0:n], in0=st[:, 0:n], in1=ps[bu][:, 0:n], op=ALU.mult),
                            reads=st.r + [psr[bu]], writes=[actT.r[j]])
                w8_done()
            JS = NJ - 4
            for mp in range(4):
                slot = wd_slots[mp]
                if mp == 0:
                    banks = {}
                    for mm_ in range(2):
                        for si, seg in enumerate(segs):
                            n, actT = seg.n, seg.actT
                            b = K.bank()
                            banks[(mm_, si)] = b
                            K.mmg(b, [(ps[b][:, 0:n], slot[:, j, mm_ * 128:(mm_ + 1) * 128], actT[:, j, 0:n], {"rd": [actT.r[j]]})
                                      for j in range(JS)], reads=slot.r, stop_last=False)
                    for mm_ in range(2):
                        m = mm_
                        for si, seg in enumerate(segs):
                            n, xT_, actT = seg.n, seg.xT, seg.actT
                            b = banks[(mm_, si)]
                            K.mmg(b, [(ps[b][:, 0:n], slot[:, j, mm_ * 128:(mm_ + 1) * 128], actT[:, j, 0:n], {"rd": [actT.r[j]]})
                                      for j in range(JS, NJ)], reads=slot.r, first_start=False)
                            dve.op(lambda b=b, m=m, n=n, xT_=xT_: nc.vector.scalar_tensor_tensor(
                                out=xT_[:, m, 0:n], in0=ps[b][:, 0:n], scalar=0.5, in1=xT_[:, m, 0:n], op0=ALU.mult, op1=ALU.add),
                                reads=[psr[b], xT_.r[m]], writes=[xT_.r[m]])
                    wd_issue(3)
                    continue
                for mm_ in range(2):
                    m = 2 * mp + mm_
                    for seg in segs:
                        n, xT_, actT = seg.n, seg.xT, seg.actT
                        b = K.bank()
                        K.mmg(b, [(ps[b][:, 0:n], slot[:, j, mm_ * 128:(mm_ + 1) * 128], actT[:, j, 0:n], {"rd": [actT.r[j]]}) for j in range(NJ)],
                              reads=slot.r)
                        dve.op(lambda b=b, m=m, n=n, xT_=xT_: nc.vector.scalar_tensor_tensor(
                            out=xT_[:, m, 0:n], in0=ps[b][:, 0:n], scalar=0.5, in1=xT_[:, m, 0:n], op0=ALU.mult, op1=ALU.add),
                            reads=[psr[b], xT_.r[m]], writes=[xT_.r[m]])

        @contextlib.contextmanager
        def arena():
            if not getattr(K, "exit_done", False):
                K.barrier()
            st = contextlib.ExitStack()
            old = K.es_cur
            K.es_cur = st
            try:
                with st:
                    try:
                        yield
                    finally:
                        K.barrier()
                        K.exit_done = True
            finally:
                K.es_cur = old

        K.last_y_tok = None

        def ffn_stage(f, gi, segs, tail=None):
            for seg in segs[::-1]:
                norm(gi, seg, next_func=AF.Silu)
            with arena():
                ystage = Buf(K, "ystage", [128, 4, D], F32) if tail is not None else None
                for si, seg in enumerate(segs):
                    seg.actT = Buf(K, "actT", [128, NJ, seg.n], BF16, nres=NJ)
                    if si == 0 and tail is None and K.last_y_tok is not None:
                        for r_ in seg.actT.r:
                            r_.r[id(K.last_y_tok[0])] = K.last_y_tok
                    seg.stmp = [Buf(K, "stmp%d" % i, [128, seg.n], F32) for i in range(2)]
                K.wdr = Ring(K, "wd", [128, NJ, 256], BF16, 3, sems=wd_sems)
                ffn(f, segs)
                if tail is not None:
                    K.last_y_tok = tail(ystage)
            issue_copy(1)

        def proj_chunks(nchunks, segs, consumes):
            for cp in range(nchunks // 2):
                ws = w8_get()
                for cc in range(2):
                    for seg, consume in zip(segs, consumes):
                        n, hT_ = seg.n, seg.hT
                        b = K.bank()
                        K.mmg(b, [(ps[b][:, 0:n], ws[:, kc, cc * 128:(cc + 1) * 128], hT_[:, kc, 0:n], {"rd": [hT_.r[kc]]}) for kc in range(8)],
                              reads=ws.r)
                        consume(2 * cp + cc, b)
                w8_done()

        def rope_pair(bA, bB, n, outA, outB, rtmp, resA, resB, view=lambda a: a, rope=rope):
            cos, sin = rope[:, 0, 0:n], rope[:, 1, 0:n]
            t = rtmp
            dve.op(lambda: nc.vector.tensor_tensor(out=t[0][:, 0:n], in0=ps[bA][:, 0:n], in1=cos, op=ALU.mult),
                   reads=[psr[bA]] + rope.r, writes=t[0].r)
            dve.op(lambda: nc.vector.tensor_tensor(out=t[1][:, 0:n], in0=ps[bB][:, 0:n], in1=sin, op=ALU.mult),
                   reads=[psr[bB]] + rope.r, writes=t[1].r)
            dve.op(lambda: nc.vector.tensor_tensor(out=outA, in0=view(t[0][:, 0:n]), in1=view(t[1][:, 0:n]), op=ALU.subtract),
                   reads=t[0].r + t[1].r, writes=resA)
            dve.op(lambda: nc.vector.tensor_tensor(out=t[2][:, 0:n], in0=ps[bA][:, 0:n], in1=sin, op=ALU.mult),
                   reads=[psr[bA]] + rope.r, writes=t[2].r)
            dve.op(lambda: nc.vector.tensor_tensor(out=t[3][:, 0:n], in0=ps[bB][:, 0:n], in1=cos, op=ALU.mult),
                   reads=[psr[bB]] + rope.r, writes=t[3].r)
            dve.op(lambda: nc.vector.tensor_tensor(out=outB, in0=view(t[2][:, 0:n]), in1=view(t[3][:, 0:n]), op=ALU.add),
                   reads=t[2].r + t[3].r, writes=resB)

        def gates_and_wo(items):
            for mp in range(4):
                wgp = w8_get()
                wga = w8_get()
                wpb = w8_get()
                wab = w8_get()
                for mm_ in range(2):
                    m = 2 * mp + mm_
                    cs = slice(mm_ * 128, (mm_ + 1) * 128)
                    for (seg, poolT, attnT, mergedT, gt) in items:
                        n, hT_ = seg.n, seg.hT
                        b1 = K.bank()
                        K.mmg(b1, [(ps[b1][:, 0:n], wgp[:, kc, cs], hT_[:, kc, 0:n], {}) for kc in range(8)], reads=hT_.r + wgp.r)
                        b2 = K.bank()
                        K.mmg(b2, [(ps[b2][:, 0:n], wga[:, kc, cs], hT_[:, kc, 0:n], {}) for kc in range(8)], reads=hT_.r + wga.r)
                        b3 = K.bank()
                        K.mmg(b3, [(ps[b3][:, 0:n], wpb[:, kc, cs], poolT[:, kc, 0:n], {}) for kc in range(4)], reads=poolT.r + wpb.r)
                        b4 = K.bank()
                        K.mmg(b4, [(ps[b4][:, 0:n], wab[:, kc, cs], attnT[:, kc, 0:n], {}) for kc in range(2)], reads=attnT.r + wab.r)
                        act.op(lambda b1=b1, n=n, gt=gt: nc.scalar.activation(out=gt[0][:, 0:n], in_=ps[b1][:, 0:n], func=AF.Sigmoid),
                               reads=[psr[b1]], writes=gt[0].r)
                        act.op(lambda b2=b2, n=n, gt=gt: nc.scalar.activation(out=gt[1][:, 0:n], in_=ps[b2][:, 0:n], func=AF.Sigmoid),
                               reads=[psr[b2]], writes=gt[1].r)
                        dve.op(lambda b3=b3, n=n, gt=gt: nc.vector.tensor_tensor(out=gt[0][:, 0:n], in0=gt[0][:, 0:n], in1=ps[b3][:, 0:n],
                                                                                 op=ALU.mult), reads=gt[0].r + [psr[b3]], writes=gt[0].r)
                        dve.op(lambda b4=b4, n=n, gt=gt: nc.vector.tensor_tensor(out=gt[1][:, 0:n], in0=gt[1][:, 0:n], in1=ps[b4][:, 0:n],
                                                                                 op=ALU.mult), reads=gt[1].r + [psr[b4]], writes=gt[1].r)
                        dve.op(lambda m=m, n=n, gt=gt, mergedT=mergedT: nc.vector.tensor_tensor(
                            out=mergedT[:, m, 0:n], in0=gt[0][:, 0:n], in1=gt[1][:, 0:n], op=ALU.add),
                            reads=gt[0].r + gt[1].r, writes=[mergedT.r[m]])
                w8_done()
            for mp in range(4):
                wo = w8_get()
                for mm_ in range(2):
                    m = 2 * mp + mm_
                    for (seg, poolT, attnT, mergedT, gt) in items:
                        n, xT_ = seg.n, seg.xT
                        b = K.bank()
                        K.mmg(b, [(ps[b][:, 0:n], wo[:, kc, mm_ * 128:(mm_ + 1) * 128], mergedT[:, kc, 0:n], {"rd": [mergedT.r[kc]]}) for kc in range(8)],
                              reads=wo.r)
                        dve.op(lambda b=b, m=m, n=n, xT_=xT_: nc.vector.tensor_tensor(out=xT_[:, m, 0:n], in0=ps[b][:, 0:n],
                                                                                      in1=xT_[:, m, 0:n], op=ALU.add),
                               reads=[psr[b], xT_.r[m]], writes=[xT_.r[m]])
                w8_done()

        K.marks = []

        def mark(name):
            K.marks.append((name, getattr(pe, "nops", 0)))
        K.mark = mark

        def dbg(level):
            return cfg.get("stop", 99) <= level

        def tp(kbase, obase):
            if kbase == 96 or obase == 96:
                return {"tile_position": (kbase, obase)}
            return {}

        def mixer_prompt(s, t, shared):
            n = T
            norm(1, SP)
            with arena():
                uext = Buf(K, "uext", [128, 4, 15 + T], F32, nres=4)
                wtmp = [Buf(K, "wtmp%d" % i, [128, 15 + T], F32) for i in range(2)]
                pT = Buf(K, "pT", [128, 4, T], BF16, nres=4)
                poolT = Buf(K, "poolT", [128, 4, T], BF16)
                qT = Buf(K, "qT", [128, 6, T], BF16, nres=6)
                rtmp = [Buf(K, "rtmp%d" % i, [128, T], F32) for i in range(4)]
                kf = Buf(K, "kf", [128, 2, T], F32, nres=2)
                ktok = Buf(K, "ktok", [128, 4, 256], F32)
                vtok = Buf(K, "vtok", [128, 4, 256], F32)
                vtok2 = Buf(K, "vtok2", [128, 4, 256], F32)
                pbuf = [Buf(K, "pbuf%d" % i, [128, T], BF16) for i in range(3)]
                attnT = Buf(K, "attnT", [128, 2, T], BF16)
                rden = Buf(K, "rden", [128, 64], F32)
                mergedT = Buf(K, "mergedT", [128, 8, T], BF16, nres=8)
                gt = [Buf(K, "gt%d" % i, [128, T], F32) for i in range(2)]
                wv = Buf(K, "wv", [128, 8, 768], BF16)
                ptok = Buf(K, "ptok", [16, 512], F32)
                hT16 = Buf(K, "hT16", [128, 8, T], BF16)
                shared.update(dict(wv=wv, ptok=ptok, ktok=ktok, rtmp=rtmp, hT16=hT16, mergedT=mergedT, pbuf=pbuf))
                hT4 = mergedT
                for kc in range(8):
                    K.copy(K.evac_eng(), hT4[:, kc, :].rearrange("p (r i) -> p i r", r=4),
                           hT[:, kc, :].rearrange("p (i r) -> p i r", r=4), hT.r, hT4.r)
                    K.copy(K.evac_eng(), hT16[:, kc, :].rearrange("p (r i) -> p i r", r=16),
                           hT[:, kc, :].rearrange("p (i r) -> p i r", r=16), hT.r, hT16.r)
                wload(K.pass_idx == 0, wv, wv[:], win_d.rearrange("(kc p) n -> p kc n", p=128)[:, :, OFF_V:OFF_V + 768],
                      wscrv.rearrange("p (k n) -> p k n", n=768), scrv_res, wv_sem3, True)
                sp.dma(rope[:], rope_d[t].rearrange("a p n -> p a n"), ds_rope, writes=rope.r)
                if t == 0:
                    dve.op(lambda: nc.vector.memset(ucarry[:], 0.0), writes=ucarry.r)
                for g in range(4):
                    dve.op(lambda g=g: nc.vector.tensor_copy(out=uext[:, g, 0:15], in_=ucarry[:, g, :]), reads=ucarry.r,
                           writes=[uext.r[g]])

                def cons_u(g, b):
                    act.op(lambda: nc.scalar.copy(out=uext[:, g, 15:15 + n], in_=ps[b][:, 0:n]),
                           reads=[psr[b]], writes=[uext.r[g]])
                yield ("u", cons_u)
                for g in range(4):
                    dve.op(lambda g=g: nc.vector.tensor_copy(out=ucarry[:, g, :], in_=uext[:, g, T:T + 15]), reads=[uext.r[g]],
                           writes=ucarry.r)
                if t == NTILE - 1:
                    b = K.bank()
                    K.mmg(b, [(ps[b][0:15, g * 128:(g + 1) * 128], uext[:, g, T:T + 15], ident[:], {"start": True, "stop": True,
                                                                                                      "is_transpose": True})
                              for g in range(4)], reads=uext.r + ident.r)
                    act.op(lambda: nc.scalar.copy(out=ptok[0:15, :], in_=ps[b][0:15, :]), reads=[psr[b]],
                           writes=ptok.r)
                    st_dma('pt', poolp_d[s, :, :], ptok[0:15, :], ptok.r)
                mark("m.qk")
                def qview(g, ab):
                    if g == 0:
                        return qT[:, ab, :]
                    dil = GROUPS[g][1]
                    return qT[:, 2 * g + ab, :].rearrange("p (r i) -> p i r", r=dil)

                def k_dst(g, ab):
                    if g == 0:
                        return [(kT0[:, ab, ((4 * t + b_) % 5) * 128:((4 * t + b_) % 5 + 1) * 128], slice(b_ * 128, (b_ + 1) * 128))
                                for b_ in range(4)]
                    if g == 1:
                        base = (t % 2) * 512
                        return [(kT1[:, ab, base:base + 512].rearrange("p (r i) -> p i r", r=4), slice(0, T))]
                    v = kT2[:, ab, :].rearrange("p (r m) -> p m r", r=16)
                    return [(v[:, 32 * t:32 * t + 32, :], slice(0, T))]

                pend = {}

                def emit_kout(g):
                    need = (g == 2) or (t == NTILE - 1)
                    if not need:
                        return
                    for b_ in range(4):
                        b = K.bank()
                        K.mmg(b, [(ps[b][:, ab * 128:(ab + 1) * 128], kf[:, ab, b_ * 128:(b_ + 1) * 128], ident[:],
                                   {"start": True, "stop": True, "is_transpose": True}) for ab in range(2)], reads=kf.r + ident.r)
                        act.op(lambda b=b, b_=b_: nc.scalar.copy(
                            out=ktok[:, b_, :].rearrange("p (h ab i) -> p ab h i", ab=2, i=32),
                            in_=ps[b][:, 0:256].rearrange("p (ab h i) -> p ab h i", ab=2, i=32)),
                            reads=[psr[b]], writes=ktok.r)
                    W = GROUPS[g][0]
                    keep = min(W, SEQ)
                    r0 = t * T - (SEQ - keep)
                    if g == 0:
                        st_dma('kt', kvp_d[0][s, :, 0, :], ktok[:, 3, :], ktok.r)
                    else:
                        st_dma('kt', kvp_d[g][s, r0:r0 + T, 0, :].rearrange("(b p) f -> p b f", p=128), ktok[:, :, :], ktok.r)

                def cons_qk(idx, b):
                    isk = idx >= 6
                    g = (idx % 6) // 2
                    ab = idx % 2
                    if ab == 0:
                        pend["A"] = b
                        return
                    bA, bB = pend["A"], b
                    if "kout" in pend:
                        emit_kout(pend.pop("kout"))
                    if not isk:
                        vw = (lambda a: a) if g == 0 else (lambda a, g=g: a.rearrange("p (i r) -> p i r", r=GROUPS[g][1]))
                        rope_pair(bA, bB, n, qview(g, 0) if g else qT[:, 0, :], qview(g, 1) if g else qT[:, 1, :], rtmp,
                                  [qT.r[2 * g]], [qT.r[2 * g + 1]], view=vw)
                    else:
                        rope_pair(bA, bB, n, kf[:, 0, :], kf[:, 1, :], rtmp, [kf.r[0]], [kf.r[1]])
                        kres = [kT0.r, kT1.r, kT2.r][g]
                        for ab_ in range(2):
                            for dst, cols in k_dst(g, ab_):
                                src = kf[:, ab_, cols]
                                if g > 0:
                                    src = src.rearrange("p (i r) -> p i r", r=GROUPS[g][1])
                                act.op(lambda dst=dst, src=src: nc.scalar.copy(out=dst, in_=src),
                                       reads=[kf.r[ab_]], writes=kres)
                        pend["kout"] = g
                yield ("qk", cons_qk)

                mark("m.V")
                def v_evac(b, rows, ncols, dst_bf, dst_f32, res_bf):
                    act.op(lambda: nc.scalar.copy(out=dst_bf, in_=ps[b][rows, 0:ncols]), reads=[psr[b]],
                           writes=res_bf)
                    if dst_f32 is not None:
                        dve.op(lambda: nc.vector.tensor_copy(out=dst_f32, in_=ps[b][rows, 0:ncols]), reads=[psr[b]], writes=vtok.r)

                last = (t == NTILE - 1)
                for bp in range(2):
                    b = K.bank()
                    for bb in range(2):
                        b_ = 2 * bp + bb
                        K.mmg(b, [(ps[b][:, bb * 256:(bb + 1) * 256], hT[:, kc, b_ * 128:(b_ + 1) * 128], wv[:, kc, 0:256], {})
                                  for kc in range(8)], reads=hT.r + wv.r)
                    for bb in range(2):
                        b_ = 2 * bp + bb
                        slot = (4 * t + b_) % 5
                        act.op(lambda b=b, bb=bb, slot=slot: nc.scalar.copy(out=V0[:, slot, :], in_=ps[b][:, bb * 256:(bb + 1) * 256]), reads=[psr[b]], writes=V0.r)
                        if last and b_ == 3:
                            dve.op(lambda b=b, bb=bb: nc.vector.tensor_copy(out=vtok[:, 0, :], in_=ps[b][:, bb * 256:(bb + 1) * 256]),
                                   reads=[psr[b]], writes=vtok.r + [psr[b]])
                            st_dma('kv', kvp_d[0][s, :, 1, :], vtok[:, 0, :], vtok.r)
                hv4 = [hT4[:, kc, :].rearrange("p (r i) -> p r i", r=4) for kc in range(8)]
                for rp in range(2):
                    b = K.bank()
                    for rr in range(2):
                        r = 2 * rp + rr
                        K.mmg(b, [(ps[b][:, rr * 256:(rr + 1) * 256], hv4[kc][:, r, :], wv[:, kc, 256:512], {}) for kc in range(8)],
                              reads=hT4.r + wv.r)
                    for rr in range(2):
                        r = 2 * rp + rr
                        act.op(lambda b=b, rr=rr, r=r: nc.scalar.copy(out=V1[:, (t % 2) * 4 + r, :],
                                                                            in_=ps[b][:, rr * 256:(rr + 1) * 256]),
                               reads=[psr[b]], writes=V1.r)
                    for rr in range(2):
                        r = 2 * rp + rr
                        if last:
                            dve.op(lambda b=b, rr=rr, r=r: nc.vector.tensor_copy(out=vtok[:, r, :], in_=ps[b][:, rr * 256:(rr + 1) * 256]),
                                   reads=[psr[b]], writes=vtok.r + [psr[b]])
                if last:
                    st_dma('kv', kvp_d[1][s, :, 1, :].rearrange("(i r) f -> i r f", r=4), vtok[:, :, :], vtok.r)
                hv16 = [hT16[:, kc, :].rearrange("p (r i) -> p r i", r=16) for kc in range(8)]
                rows = slice(32 * t, 32 * t + 32)
                for rq in range(4):
                    vt_ = vtok if rq % 2 == 0 else vtok2
                    for rp in range(2):
                        b = K.bank()
                        for rr in range(2):
                            r = 4 * rq + 2 * rp + rr
                            K.mmg(b, [(ps[b][rows, rr * 256:(rr + 1) * 256], hv16[kc][:, r, :], wv[:, kc, 512:768], tp(0, 32 * t))
                                      for kc in range(8)], reads=hT16.r + wv.r)
                        ev = act if rp == 0 else dve
                        for rr in range(2):
                            r = 4 * rq + 2 * rp + rr
                            K.copy(ev, V2[rows, r, :], ps[b][rows, rr * 256:(rr + 1) * 256], [psr[b]], V2.r)
                            K.copy(ev, vt_[rows, r % 4, :], ps[b][rows, rr * 256:(rr + 1) * 256], [psr[b]], vt_.r)
                    dst = kvp_d[2][s, t * T:(t + 1) * T, 1, :].rearrange("(i r) f -> i r f", r=16)[:, 4 * rq:4 * rq + 4, :]
                    st_dma('kv' if rq % 2 == 0 else 'kv2', dst, vt_[rows, :, :], vt_.r)

                if "kout" in pend:
                    emit_kout(pend.pop("kout"))
                mark("m.pool")
                L = 15 + T
                for g in range(4):
                    src = uext[:, g, :]
                    cur = None
                    sh = 1
                    for step in range(g + 1):
                        dst = wtmp[step % 2]
                        srcap = src if cur is None else cur[:, :]
                        rd = [uext.r[g]] if cur is None else cur.r
                        pe_, pn_ = (dve, nc.vector) if K.pass_idx == 0 else (pool, nc.gpsimd)
                        pe_.op(lambda dst=dst, srcap=srcap, sh=sh, pn_=pn_: pn_.tensor_tensor(
                            out=dst[:, 2 * sh - 1:L], in0=srcap[:, 2 * sh - 1:L], in1=srcap[:, sh - 1:L - sh], op=ALU.add), reads=rd, writes=dst.r)
                        cur = dst
                        sh *= 2
                    w = 2 ** (g + 1)
                    if K.pass_idx == 0:
                        dve.op(lambda g=g, cur=cur, w=w: nc.vector.scalar_tensor_tensor(
                            out=pT[:, g, :], in0=cur[:, 15:L], scalar=1.0 / w, in1=uext[:, g, 15:L], op0=ALU.mult, op1=ALU.subtract),
                            reads=cur.r + [uext.r[g]], writes=[pT.r[g]])
                        if t == 0:
                            dve.op(lambda cur=cur, g=g: nc.vector.tensor_tensor(out=cur[:, 15:31], in0=cur[:, 15:31], in1=rcnt[:, g, :],
                                                                                op=ALU.mult), reads=cur.r + rcnt.r, writes=cur.r)
                            dve.op(lambda cur=cur, g=g: nc.vector.tensor_tensor(out=pT[:, g, 0:16], in0=cur[:, 15:31],
                                                                                in1=uext[:, g, 15:31], op=ALU.subtract),
                                   reads=cur.r + [uext.r[g]], writes=[pT.r[g]])
                    else:
                        fx = rden[:, g * 16:(g + 1) * 16]
                        if t == 0:
                            pool.op(lambda cur=cur, g=g, fx=fx: nc.gpsimd.tensor_tensor(out=fx, in0=cur[:, 15:31], in1=rcnt[:, g, :],
                                                                                        op=ALU.mult), reads=cur.r + rcnt.r, writes=rden.r)
                        pool.op(lambda cur=cur, w=w: nc.gpsimd.tensor_scalar(
                            cur[:, 15:L], cur[:, 15:L], 1.0 / w, 0.0, op0=ALU.mult, op1=ALU.add), reads=cur.r, writes=cur.r)
                        pool.op(lambda g=g, cur=cur: nc.gpsimd.tensor_tensor(out=pT[:, g, :], in0=cur[:, 15:L], in1=uext[:, g, 15:L],
                                                                             op=ALU.subtract), reads=cur.r + [uext.r[g]], writes=[pT.r[g]])
                        if t == 0:
                            pool.op(lambda g=g, fx=fx: nc.gpsimd.tensor_tensor(out=pT[:, g, 0:16], in0=fx, in1=uext[:, g, 15:31],
                                                                               op=ALU.subtract), reads=rden.r + [uext.r[g]], writes=[pT.r[g]])

                def pool_mm():
                    for g in range(4):
                        b = K.bank()
                        K.mmg(b, [(ps[b][:, 0:n], wgrp[:, g, :], pT[:, g, :], {})], reads=[pT.r[g]] + wgrp.r)
                        act.op(lambda g=g, b=b: nc.scalar.mul(poolT[:, g, :], ps[b][:, 0:n], pscale[:, g:g + 1]),
                               reads=[psr[b]] + pscale.r, writes=poolT.r)

                mark("m.attn")
                K.ps_free = [0, 1, 2, 3]
                numb = [4, 5]
                denb = [6, 7]
                for b in numb + denb:
                    K.mmg(b, [(ps[b][:, :], zeros[0:1, 0:128], zeros[0:1, 0:512], {"start": True, "stop": False, "skip_group_check": True})],
                          reads=zeros.r)
                pi = [0]

                def softmax_block(b, krows, c0, ncols, mask_ap, mask_rows, mres):
                    pb = pbuf[pi[0] % 3]
                    pi[0] += 1
                    act.op(lambda: nc.scalar.activation(out=pb[0:krows, c0:c0 + ncols], in_=ps[b][0:krows, c0:c0 + ncols], func=AF.Exp,
                                                        scale=0.125), reads=[psr[b]], writes=pb.r)
                    if mask_ap is not None:
                        dve.op(lambda: nc.vector.tensor_tensor(out=pb[mask_rows, c0:c0 + ncols], in0=pb[mask_rows, c0:c0 + ncols],
                                                               in1=mask_ap, op=ALU.mult), reads=pb.r + mres, writes=pb.r)
                    return pb

                def unit01(g, h, which, units):
                    kTg, Vg = (kT0, V0) if g == 0 else (kT1, V1)
                    c, half = h // 2, (h % 2) * 64
                    b = K.bank()
                    items = []
                    for (u_, slot) in units:
                        for ab in range(2):
                            items.append((ps[b][:, u_ * 128:(u_ + 1) * 128], kTg[32 * h:32 * h + 32, ab, slot * 128:(slot + 1) * 128],
                                          qT[32 * h:32 * h + 32, 2 * g + ab, u_ * 128:(u_ + 1) * 128],
                                          dict(start=(ab == 0), stop=(ab == 1), **tp(32 * h, 0))))
                    K.mmg(b, items, reads=kTg.r + [qT.r[2 * g], qT.r[2 * g + 1]])
                    c0 = units[0][0] * 128
                    ncols = len(units) * 128
                    mi = 0 if which == "prev" else 1
                    pb = softmax_block(b, 128, c0, ncols, mask01[:, mi, 0:len(units), :].rearrange("p h i -> p (h i)"),
                                       slice(0, 128), mask01.r)
                    yield
                    nu = len(units)
                    for ui, (u_, slot) in enumerate(units):
                        for isden in (False, True):
                            bb = denb[c] if isden else numb[c]
                            lhsT = ones[:, 0:64] if isden else Vg[:, slot, h * 64:(h + 1) * 64]
                            if g == 0:
                                outap = ps[bb][half:half + 64, u_ * 128:(u_ + 1) * 128]
                            else:
                                outap = ps[bb][half:half + 64, :].rearrange("p (i r) -> p r i", r=4)[:, u_, :]
                            first = (ui == 0 and not isden)
                            lastm = (ui == nu - 1 and isden)
                            pe.op(lambda outap=outap, lhsT=lhsT, pb=pb, u_=u_: nc.tensor.matmul(
                                outap, lhsT, pb[:, u_ * 128:(u_ + 1) * 128], start=False, stop=False, skip_group_check=True),
                                reads=Vg.r + pb.r + ones.r if first else (), writes=(), inc=lastm)
                    tok = (pe.sem, pe.cnt)
                    EngW.commit(tok, Vg.r + pb.r, [psr[x_] for x_ in numb + denb])

                kr = 32 * (t + 1)

                def unit2(h):
                    b = K.bank()
                    items = []
                    for r in range(16):
                        for ab in range(2):
                            items.append((ps[b][0:kr, r * 32:(r + 1) * 32], kT2[32 * h:32 * h + 32, ab, r * 128:r * 128 + kr],
                                          qT[32 * h:32 * h + 32, 4 + ab, r * 32:(r + 1) * 32],
                                          dict(start=(ab == 0), stop=(ab == 1), **tp(32 * h, 0))))
                    K.mmg(b, items, reads=kT2.r + [qT.r[4], qT.r[5]])
                    pb = softmax_block(b, kr, 0, 512, mask2[32 * t:32 * t + 32, :, :].rearrange("p r i -> p (r i)"),
                                       slice(32 * t, 32 * t + 32), mask2.r)
                    yield
                    c, half = h // 2, (h % 2) * 64
                    for r in range(16):
                        for bank_list, isden in ((numb, False), (denb, True)):
                            bb = bank_list[c]
                            lhsT = ones[0:kr, 0:64] if isden else V2[0:kr, r, h * 64:(h + 1) * 64]
                            outap = ps[bb][half:half + 64, :].rearrange("p (i r) -> p r i", r=16)[:, r, :]
                            pe.op(lambda outap=outap, lhsT=lhsT, pb=pb, r=r, h=h: nc.tensor.matmul(
                                outap, lhsT, pb[0:kr, r * 32:(r + 1) * 32], start=False, stop=False, skip_group_check=True),
                                reads=V2.r + pb.r + ones.r if (r == 0 and not isden) else (), writes=(), inc=(r == 15 and isden))
                    tok = (pe.sem, pe.cnt)
                    EngW.commit(tok, V2.r + pb.r, [psr[x_] for x_ in numb + denb])

                makers = []
                for g in (0, 1):
                    for h in range(4):
                        for which in ("cur", "prev"):
                            units = []
                            for u_ in range(4):
                                if g == 0:
                                    babs = 4 * t + u_
                                    if which == "cur":
                                        units.append((u_, babs % 5))
                                    elif babs > 0:
                                        units.append((u_, (babs - 1) % 5))
                                else:
                                    if which == "cur":
                                        units.append((u_, (t % 2) * 4 + u_))
                                    elif t > 0:
                                        units.append((u_, ((t - 1) % 2) * 4 + u_))
                            if units:
                                makers.append(lambda g=g, h=h, which=which, units=units: unit01(g, h, which, units))
                for h in range(4):
                    makers.append(lambda h=h: unit2(h))
                DEPTH = 2
                live = []
                for mk in makers:
                    gen = mk()
                    next(gen)
                    live.append(gen)
                    if len(live) > DEPTH:
                        for _ in live.pop(0):
                            pass
                for gen in live:
                    for _ in gen:
                        pass
                for c in range(2):
                    rd_ = gt[c]
                    act.op(lambda c=c, rd_=rd_: nc.scalar.activation(out=rd_[:, :], in_=ps[denb[c]][:, :], func=AF.Ln), reads=[psr[denb[c]]],
                           writes=rd_.r)
                    act.op(lambda rd_=rd_: nc.scalar.activation(out=rd_[:, :], in_=rd_[:, :], func=AF.Exp, scale=-1.0), reads=rd_.r, writes=rd_.r)
                    dve.op(lambda c=c, rd_=rd_: nc.vector.tensor_tensor(out=attnT[:, c, :], in0=ps[numb[c]][:, :], in1=rd_[:, :], op=ALU.mult),
                           reads=[psr[numb[c]]] + rd_.r, writes=attnT.r)
                K.ps_free = list(range(8))

                pool_mm()
                mark("m.gates")
                yield ("gates", (SP, poolT, attnT, mergedT, gt))

        K.final_sems = [st_sems[i][k] for i in range(2) for k in ("kv", "kv2", "kt", "pt", "y", "ys")]

        def mixer_sample(shared):
            n = NS
            hT = hTs
            norm(1, SS)
            with arena():
                unew = Buf(K, "unew", [128, 4, NS], F32)
                uexs = Buf(K, "uexs", [128, 4, NS, 16], F32)
                al = bool(shared)
                sptok = shared["ptok"] if al else Buf(K, "sptok", [16, 512], F32)
                wsum = Buf(K, "wsum", [128, 4, NS], F32)
                pT = Buf(K, "pTs", [128, 4, NS], BF16)
                poolT = Buf(K, "poolTs", [128, 4, NS], BF16)
                qT = Buf(K, "qTs", [128, 6, NS], BF16)
                rtmp = [Buf(K, "rtmps%d" % i, [128, NS], F32) for i in range(4)]
                big = shared["rtmp"] if al else [Buf(K, "bigs%d" % i, [128, 512], F32) for i in range(3)]
                kf = Buf(K, "kfs", [128, 6, NS], F32)
                kTs = Buf(K, "kTs", [128, 6, NS], BF16)
                ktok = shared["ktok"] if al else Buf(K, "ktoks", [128, 4, 256], F32)
                utok = Buf(K, "utoks", [NS, 512], F32)
                vrow_f = Buf(K, "vrowf", [1, 768], F32)
                vrow_bb = shared["hT16"] if al else Buf(K, "vrowb", [128, 8, T], BF16)
                vrow_b_ap = lambda bb, c0, c1: vrow_bb[0:1, 0:6, :].rearrange("p a c -> p (a c)")[:, bb * 768 + c0:bb * 768 + c1]
                kctok = big[0:2]
                kcperm = big[2]
                kcT = Buf(K, "kcT", [128, 2, 128], BF16)
                vcbb = shared["mergedT"] if al else Buf(K, "vcb", [128, 8, T], BF16)
                vcb_ap = lambda idx, c0=0, c1=256: vcbb[:, 0:6, :].rearrange("p a (b c) -> p (a b) c", c=256)[:, idx, c0:c1]
                pTh = [Buf(K, "pTh%d" % h, [128, 12], BF16) for h in range(4)]
                pself = [Buf(K, "pself%d" % h, [1, 12], BF16) for h in range(4)]
                attnT = Buf(K, "attnTs", [128, 2, NS], BF16)
                rden = Buf(K, "rdens", [128, 8], F32)
                mergedT = Buf(K, "mergedTs", [128, 8, NS], BF16, nres=8)
                gt = [Buf(K, "gts%d" % i, [128, NS], F32) for i in range(2)]
                wv = shared["wv"] if al else Buf(K, "wvs", [128, 8, 768], BF16)
                ds_kc = [K.dsem("ds_kc%d" % i) for i in range(2)]
                K.arena_dsems.extend(ds_kc)

                for tk in K.arena_toks:
                    sp.wait_tok(tk)
                if not al:
                    wload(K.pass_idx == 0, wv, wv[:], win_d.rearrange("(kc p) n -> p kc n", p=128)[:, :, OFF_V:OFF_V + 768],
                          wscrv.rearrange("p (k n) -> p k n", n=768), scrv_res, wv_sem3, True)
                for bb in range(NS):
                    sp.dma(sptok[0:15, :], spool_d[bb, :, :], ds_sp, writes=sptok.r)
                    b = K.bank()
                    K.mmg(b, [(ps[b][:, g * 16:g * 16 + 15], sptok[0:15, g * 128:(g + 1) * 128], ident[0:15, 0:15],
                               {"start": True, "stop": True, "is_transpose": True}) for g in range(4)], reads=sptok.r + ident.r)
                    act.op(lambda b=b, bb=bb: nc.scalar.copy(out=uexs[:, :, bb, 0:15],
                                                             in_=ps[b][:, 0:64].rearrange("p (g r) -> p g r", r=16)[:, :, 0:15]),
                           reads=[psr[b]], writes=uexs.r)

                def cons_u(g, b):
                    act.op(lambda: nc.scalar.copy(out=unew[:, g, :], in_=ps[b][:, 0:n]), reads=[psr[b]], writes=unew.r)
                yield ("u", cons_u)
                dve.op(lambda: nc.vector.tensor_copy(out=uexs[:, :, :, 15], in_=unew[:, :, :]), reads=unew.r, writes=uexs.r)
                b = K.bank()
                K.mmg(b, [(ps[b][0:NS, g * 128:(g + 1) * 128], unew[:, g, :], ident[:], {"start": True, "stop": True, "is_transpose": True})
                          for g in range(4)], reads=unew.r + ident.r)
                act.op(lambda: nc.scalar.copy(out=utok[:, :], in_=ps[b][0:NS, :]), reads=[psr[b]], writes=utok.r)
                st_dma('pt', pools_d[:, 14, :], utok[:, :], utok.r)
                for g in range(4):
                    w = 2 ** (g + 1)
                    dve.op(lambda g=g, w=w: nc.vector.tensor_reduce(out=wsum[:, g, :], in_=uexs[:, g, :, 16 - w:16],
                                                                    axis=mybir.AxisListType.X, op=ALU.add),
                           reads=uexs.r, writes=wsum.r)
                    dve.op(lambda g=g, w=w: nc.vector.scalar_tensor_tensor(out=pT[:, g, :], in0=wsum[:, g, :], scalar=1.0 / w,
                                                                           in1=unew[:, g, :], op0=ALU.mult, op1=ALU.subtract),
                           reads=wsum.r + unew.r, writes=pT.r)
                    b = K.bank()
                    K.mmg(b, [(ps[b][:, 0:n], wgrp[:, g, :], pT[:, g, :], {})], reads=pT.r + wgrp.r)
                    act.op(lambda g=g, b=b: nc.scalar.mul(poolT[:, g, :], ps[b][:, 0:n], pscale[:, g:g + 1]),
                           reads=[psr[b]] + pscale.r, writes=poolT.r)
                pend = {}

                def cons_qk(idx, b):
                    isk = idx >= 6
                    g = (idx % 6) // 2
                    ab = idx % 2
                    if ab == 0:
                        pend["A"] = b
                        return
                    bA, bB = pend["A"], b
                    if not isk:
                        rope_pair(bA, bB, n, qT[:, 2 * g, :], qT[:, 2 * g + 1, :], rtmp, qT.r, qT.r, rope=rope_s)
                    else:
                        rope_pair(bA, bB, n, kf[:, 2 * g, :], kf[:, 2 * g + 1, :], rtmp, kf.r, kf.r, rope=rope_s)
                yield ("qk", cons_qk)
                act.op(lambda: nc.scalar.copy(out=kTs[:, :, :], in_=kf[:, :, :]), reads=kf.r, writes=kTs.r)
                b = K.bank()
                K.mmg(b, [(ps[b][0:NS, cc * 128:(cc + 1) * 128], kf[:, cc, :], ident[:], {"start": True, "stop": True, "is_transpose": True})
                          for cc in range(4)], reads=kf.r + ident.r)
                b2 = K.bank()
                K.mmg(b2, [(ps[b2][0:NS, (cc - 4) * 128:(cc - 3) * 128], kf[:, cc, :], ident[:], {"start": True, "stop": True, "is_transpose": True})
                           for cc in range(4, 6)], reads=kf.r + ident.r)
                for g in range(3):
                    bk, c0 = (b, g * 256) if g < 2 else (b2, 0)
                    act.op(lambda g=g, bk=bk, c0=c0: nc.scalar.copy(
                        out=ktok[0:NS, g, :].rearrange("p (h ab i) -> p ab h i", ab=2, i=32),
                        in_=ps[bk][0:NS, c0:c0 + 256].rearrange("p (ab h i) -> p ab h i", ab=2, i=32)),
                        reads=[psr[bk]], writes=ktok.r)
                for g in range(3):
                    Lg = GROUPS[g][0]
                    st_dma('kt', kvs_d[g][:, Lg - 1, 0, :], ktok[0:NS, g, :], ktok.r)
                for bb in range(NS):
                    b = K.bank()
                    K.mmg(b, [(ps[b][0:1, 0:512], hT[:, kc, bb:bb + 1], wv[:, kc, 0:512], {}) for kc in range(8)], reads=hT.r + wv.r)
                    b2 = K.bank()
                    K.mmg(b2, [(ps[b2][0:1, 0:256], hT[:, kc, bb:bb + 1], wv[:, kc, 512:768], {}) for kc in range(8)], reads=hT.r + wv.r)
                    act.op(lambda b=b, bb=bb: nc.scalar.copy(out=vrow_f[0:1, 0:512], in_=ps[b][0:1, 0:512]), reads=[psr[b]], writes=vrow_f.r)
                    act.op(lambda b2=b2, bb=bb: nc.scalar.copy(out=vrow_f[0:1, 512:768], in_=ps[b2][0:1, 0:256]), reads=[psr[b2]],
                           writes=vrow_f.r)
                    dve.op(lambda bb=bb: nc.vector.tensor_copy(out=vrow_b_ap(bb, 0, 768), in_=vrow_f[0:1, :]), reads=vrow_f.r,
                           writes=vrow_bb.r)
                    for g in range(3):
                        Lg = GROUPS[g][0]
                        st_dma('kv', kvs_d[g][bb, Lg - 1, 1:2, :], vrow_f[0:1, g * 256:(g + 1) * 256], vrow_f.r)
                K.ps_free = [0, 1, 2]
                hb = [3, 4, 5, 6]
                accb = 7
                K.mmg(accb, [(ps[accb][:, :], zeros[0:1, 0:128], zeros[0:1, 0:512], {"start": True, "stop": False, "skip_group_check": True})],
                      reads=zeros.r)
                ci = 0
                for bb in range(NS):
                    for g in range(3):
                        Lg, dil = GROUPS[g]
                        idx = bb * 3 + g
                        kc_ = kctok[ci % 2]
                        dsm = ds_kc[ci % 2]
                        ci += 1
                        sp.dma(kc_[:, :].rearrange("p (t f) -> p t f", t=2), cache_d[g][bb, 0:Lg:dil, :, :], dsm, writes=kc_.r)
                        dve.op(lambda kc_=kc_: nc.vector.tensor_copy(out=kcperm[:, 0:256].rearrange("p (ab h i) -> p ab h i", ab=2, i=32),
                                                                     in_=kc_[:, 0:256].rearrange("p (h ab i) -> p ab h i", ab=2, i=32)),
                               reads=kc_.r, writes=kcperm.r)
                        act.op(lambda kc_=kc_, idx=idx: nc.scalar.copy(out=vcb_ap(idx), in_=kc_[:, 256:512]), reads=kc_.r, writes=vcbb.r)
                        b = K.bank()
                        K.mmg(b, [(ps[b][:, ab * 128:(ab + 1) * 128], kcperm[:, ab * 128:(ab + 1) * 128], ident[:],
                                   {"start": True, "stop": True, "is_transpose": True}) for ab in range(2)], reads=kcperm.r + ident.r)
                        dve.op(lambda b=b: nc.vector.tensor_copy(out=kcT[:, :, :].rearrange("p a k -> p (a k)"), in_=ps[b][:, 0:256]),
                               reads=[psr[b]], writes=kcT.r)
                        for h in range(4):
                            K.mmg(hb[h], [(ps[hb[h]][:, idx:idx + 1], kcT[32 * h:32 * h + 32, ab, :], qT[32 * h:32 * h + 32, 2 * g + ab, bb:bb + 1],
                                           dict(start=(ab == 0), stop=(ab == 1), **tp(32 * h, 0))) for ab in range(2)],
                                  reads=kcT.r + qT.r)
                            K.mmg(hb[h], [(ps[hb[h]][0:1, 16 + idx:17 + idx], kTs[32 * h:32 * h + 32, 2 * g + ab, bb:bb + 1],
                                           qT[32 * h:32 * h + 32, 2 * g + ab, bb:bb + 1],
                                           dict(start=(ab == 0), stop=(ab == 1), **tp(32 * h, 0))) for ab in range(2)],
                                  reads=kTs.r + qT.r)
                for h in range(4):
                    act.op(lambda h=h: nc.scalar.activation(out=pTh[h][:, :], in_=ps[hb[h]][:, 0:12], func=AF.Exp, scale=0.125),
                           reads=[psr[hb[h]]], writes=pTh[h].r)
                    act.op(lambda h=h: nc.scalar.activation(out=pself[h][0:1, :], in_=ps[hb[h]][0:1, 16:28], func=AF.Exp, scale=0.125),
                           reads=[psr[hb[h]]], writes=pself[h].r)
                for h in range(4):
                    c, half = h // 2, (h % 2) * 64
                    n_mm = 0
                    for bb in range(NS):
                        for g in range(3):
                            idx = bb * 3 + g
                            for isden in (False, True):
                                col = (2 * isden + c) * 4 + bb
                                outap = ps[accb][half:half + 64, col:col + 1]
                                l1 = ones[:, 0:64] if isden else vcb_ap(idx, h * 64, (h + 1) * 64)
                                l2 = ones[0:1, 0:64] if isden else vrow_b_ap(bb, g * 256 + h * 64, g * 256 + (h + 1) * 64)
                                first = (n_mm == 0)
                                pe.op(lambda outap=outap, l1=l1, h=h, idx=idx: nc.tensor.matmul(
                                    outap, l1, pTh[h][:, idx:idx + 1], start=False, stop=False, skip_group_check=True),
                                    reads=vcbb.r + vrow_bb.r + pTh[h].r + pself[h].r + ones.r if first else (), writes=(), inc=False)
                                lastm = (bb == NS - 1 and g == 2 and isden)
                                pe.op(lambda outap=outap, l2=l2, h=h, idx=idx: nc.tensor.matmul(
                                    outap, l2, pself[h][0:1, idx:idx + 1], start=False, stop=False, skip_group_check=True),
                                    reads=(), writes=(), inc=lastm)
                                n_mm += 1
                    tok = (pe.sem, pe.cnt)
                    EngW.commit(tok, vcbb.r + vrow_bb.r + pTh[h].r + pself[h].r, [psr[accb]])
                dve.op(lambda: nc.vector.reciprocal(out=rden[:, :], in_=ps[accb][:, 8:16]), reads=[psr[accb]], writes=rden.r)
                for c in range(2):
                    dve.op(lambda c=c: nc.vector.tensor_tensor(out=attnT[:, c, :], in0=ps[accb][:, c * 4:(c + 1) * 4], in1=rden[:, c * 4:(c + 1) * 4],
                                                               op=ALU.mult), reads=[psr[accb]] + rden.r, writes=attnT.r)
                K.ps_free = list(range(8))
                yield ("gates", (SS, poolT, attnT, mergedT, gt))
                for d in ds_kc:
                    K.arena_dsems.remove(d)

        def run_mixer(s_, t_, segs):
            shared = {}
            gens = []
            if SP in segs:
                gens.append(mixer_prompt(s_, t_, shared))
            if SS in segs:
                gens.append(mixer_sample(shared))
            r = [next(g) for g in gens]
            assert all(x[0] == "u" for x in r)
            proj_chunks(4, segs, [x[1] for x in r])
            r = [next(g) for g in gens]
            assert all(x[0] == "qk" for x in r)
            proj_chunks(12, segs, [x[1] for x in r])
            r = [next(g) for g in gens]
            assert all(x[0] == "gates" for x in r)
            gates_and_wo([x[1] for x in r])
            for g in gens[::-1]:
                for _ in g:
                    raise AssertionError("unexpected yield")

        def sample_in():
            sp.dma(xin[0:NS, 0, :], xs_d[:, :], ds_x, writes=xin.r)
            sp.dma(rope_s[:], rope_d[4].rearrange("a p n -> p a n")[:, :, 0:NS], ds_rope, writes=rope_s.r)
            b = K.bank()
            K.mmg(b, [(ps[b][:, c * 4:(c + 1) * 4], xin[0:NS, 0, c * 128:(c + 1) * 128], ident[0:NS, 0:NS],
                       {"start": True, "stop": True, "is_transpose": True}) for c in range(8)], reads=xin.r + ident.r)
            dve.op(lambda: nc.vector.tensor_copy(out=xTs[:, :, 0:NS], in_=ps[b][:, 0:32].rearrange("p (c n) -> p c n", n=NS)),
                   reads=[psr[b]], writes=xTs.r)

        K.pending_copies = []

        def sample_copies():
            for g in (2, 1, 0):
                Lg = GROUPS[g][0]
                for bb in range(NS):
                    nel = (Lg - 1) * 512
                    a = 16 if nel % 16 == 0 else 1
                    src = cache_d[g][bb, 1:Lg, :, :].rearrange("r t f -> (r t f)").rearrange("(a c) -> a c", a=a)
                    dst = kvs_d[g][bb, 0:Lg - 1, :, :].rearrange("r t f -> (r t f)").rearrange("(a c) -> a c", a=a)
                    K.pending_copies.append((dst, src))
            K.pending_copies.append((pools_d[:, 0:14, :], spool_d[:, 1:15, :]))

        def issue_copy(nmax=1):
            for _ in range(nmax):
                if K.pending_copies:
                    dst, src = K.pending_copies.pop(0)
                    (sp if K.pass_idx == 0 else pool).dma(dst, src, ds_misc)

        def sample_out():
            norm(3, SS, out_f32=True)
            ystg = rope
            b = K.bank()
            K.mmg(b, [(ps[b][0:NS, cc * 128:(cc + 1) * 128], xTs[:, cc, 0:NS], ident[:], {"start": True, "stop": True, "is_transpose": True})
                      for cc in range(4)], reads=xTs.r + ident.r)
            b2 = K.bank()
            K.mmg(b2, [(ps[b2][0:NS, (cc - 4) * 128:(cc - 3) * 128], xTs[:, cc, 0:NS], ident[:], {"start": True, "stop": True, "is_transpose": True})
                       for cc in range(4, 8)], reads=xTs.r + ident.r)
            act.op(lambda: nc.scalar.copy(out=ystg[0:NS, 0, :], in_=ps[b][0:NS, :]), reads=[psr[b]], writes=ystg.r)
            act.op(lambda: nc.scalar.copy(out=ystg[0:NS, 1, :], in_=ps[b2][0:NS, :]), reads=[psr[b2]], writes=ystg.r)
            st_dma('ys', ys_d[:, :], ystg[0:NS, :, :].rearrange("p a n -> p (a n)"), ystg.r)

        K.x_prefetched = False
        for s in range(nseq):
            for t in range(ntile):
                n = T
                K.pass_idx = s * ntile + t
                with_s = do_sample and s == 0 and t == 0
                segs = [SP, SS] if with_s else [SP]
                mark("load")
                if not K.x_prefetched:
                    sp.dma(xin[:, :, :], x_d[s, t * T:(t + 1) * T, :].rearrange("(b p) d -> p b d", p=128), ds_x, writes=xin.r)
                K.x_prefetched = False
                for c in range(8):
                    b = K.bank()
                    K.mmg(b, [(ps[b][:, b_ * 128:(b_ + 1) * 128], xin[:, b_, c * 128:(c + 1) * 128], ident[:],
                               {"start": True, "stop": True, "is_transpose": True}) for b_ in range(4)], reads=xin.r + ident.r)
                    K.copy(K.evac_eng(), xT[:, c, :], ps[b][:, :], [psr[b]], [xT.r[c]])
                if with_s:
                    sample_in()
                if K.pass_idx == 1 or (with_s and K.pass_idx == 0):
                    pass
                mark("ffn1")
                ffn_stage(0, 0, segs)
                mark("mixer")
                run_mixer(s, t, segs)
                mark("ffn2")
                nxt = s * ntile + t + 1
                if nxt < nseq * ntile:
                    s2, t2 = divmod(nxt, ntile)
                    sp.dma(xin[:, :, :], x_d[s2, t2 * T:(t2 + 1) * T, :].rearrange("(b p) d -> p b d", p=128), ds_x, writes=xin.r)
                    K.x_prefetched = True

                def tail(ystage, s=s, t=t):
                    mark("final")
                    norm(3, SP, out_f32=True)
                    for b_ in range(4):
                        for cq in range(2):
                            b = K.bank()
                            K.mmg(b, [(ps[b][:, cc * 128:(cc + 1) * 128], xT[:, 4 * cq + cc, b_ * 128:(b_ + 1) * 128], ident[:],
                                       {"start": True, "stop": True, "is_transpose": True}) for cc in range(4)],
                                  reads=[xT.r[4 * cq + cc] for cc in range(4)] + ident.r)
                            K.copy(K.evac_eng(), ystage[:, b_, cq * 512:(cq + 1) * 512], ps[b][:, :], [psr[b]], ystage.r)
                    return st_dma('y', y_d[s, t * T:(t + 1) * T, :].rearrange("(b p) d -> p b d", p=128), ystage[:, :, :], ystage.r)
                ffn_stage(1, 2, segs, tail=tail)
                if with_s:
                    sample_out()
                if do_sample and K.pass_idx == min(1, nseq * ntile - 1):
                    sample_copies()

        mark("sample")
        if do_sample and nseq * ntile == 0:
            K.pass_idx = 0
            sample_in()
            sample_copies()
            ffn_stage(0, 0, [SS])
            run_mixer(0, 0, [SS])
            ffn_stage(1, 2, [SS])
            sample_out()

        issue_copy(100)
        for d in [ds_misc] + K.final_sems:
            if d.total > 0:
                sp.wait_tok((d.sem, d.total))
    nc._marks = K.marks + [("end", getattr(K.pe, "nops", 0))]
    return nc


def _win_perm():
    cols = list(range(512))
    for which in (0, 1):
        for g in range(3):
            for half in (0, 1):
                for h in range(4):
                    base = 512 + g * 768 + which * 256 + h * 64 + half * 32
                    cols.extend(range(base, base + 32))
    for g in range(3):
        base = 512 + g * 768 + 2 * 256
        cols.extend(range(base, base + 256))
    cols.extend(range(2816, 4864))
    return np.asarray(cols, dtype=np.int64)


def _consts():
    ident = np.eye(128, dtype=np.float32)
    kk = np.arange(128)[:, None]
    ii = np.arange(128)[None, :]
    m_prev = (kk >= ii).astype(np.float32)
    m_cur = (kk <= ii).astype(np.float32)
    mask01 = np.stack([np.broadcast_to(m_prev[:, None, :], (128, 4, 128)), np.broadcast_to(m_cur[:, None, :], (128, 4, 128))], axis=1)
    m2 = ((np.arange(128)[:, None] % 32) <= np.arange(32)[None, :]).astype(np.float32)
    mask2 = np.broadcast_to(m2[:, None, :], (128, 16, 32))
    half = 32
    inv = (np.float32(10000.0) ** (-(np.arange(half, dtype=np.float32)) * np.float32(2.0 / 64))).astype(np.float32)
    rope = np.zeros((5, 2, 128, 512), np.float32)
    for t in range(5):
        pos = (np.arange(512) + 512 * t).astype(np.float32) if t < 4 else np.full(512, PAST, np.float32)
        ang = (pos[None, :] * inv[:, None]).astype(np.float32)
        rope[t, 0] = np.tile(np.cos(ang).astype(np.float32), (4, 1))
        rope[t, 1] = np.tile(np.sin(ang).astype(np.float32), (4, 1))
    rcnt = np.zeros((128, 4, 16), np.float32)
    for g, w in enumerate((2, 4, 8, 16)):
        rcnt[:, g, :] = (np.float32(1.0) / np.minimum(np.arange(16) + 1, w).astype(np.float32))[None, :]
    return dict(ident=ident, mask01=np.ascontiguousarray(mask01), mask2=np.ascontiguousarray(mask2), rope=rope, rcnt=rcnt)


def make_in_maps(inp):
    perm = _win_perm()
    c = _consts()
    f = lambda a: np.ascontiguousarray(np.asarray(a, dtype=np.float32))
    shared = dict(
        gains=f(np.stack([inp["norm_ffn1"][0], inp["norm_mix"][0], inp["norm_ffn2"][0], inp["norm_final"]])),
        pscale=f(inp["pool_scale"][0]),
        wg1=f(inp["ffn1_w_gate"][0]), wu1=f(inp["ffn1_w_up"][0]), wd1=f(inp["ffn1_w_down"][0]),
        wg2=f(inp["ffn2_w_gate"][0]), wu2=f(inp["ffn2_w_up"][0]), wd2=f(inp["ffn2_w_down"][0]),
        win=f(np.asarray(inp["w_in"][0])[:, perm]), wgrp=f(inp["w_pool_grp"][0]), wpb=f(inp["w_pool_br"][0]),
        wab=f(inp["w_attn_br"][0]), wo=f(inp["w_o"][0]), **c)
    maps = []
    for k in range(8):
        m = dict(shared)
        m["x"] = f(inp["x_prompt"][NSEQ * k:NSEQ * (k + 1)])
        m["xs"] = f(np.asarray(inp["x_sample"])[NS * k:NS * (k + 1), 0, :])
        m["spool"] = f(np.asarray(inp["state_pool"])[0, NS * k:NS * (k + 1)])
        for (w, _), nm in zip(GROUPS, ("cache_kv_w128", "cache_kv_w512", "cache_kv_w2048")):
            m["c%d" % w] = f(np.asarray(inp[nm])[0, NS * k:NS * (k + 1)].reshape(NS, w, 2, 256))
        maps.append(m)
    return maps


def assemble(results):
    cat = lambda nm: np.concatenate([r[nm] for r in results], axis=0)
    y = cat("y")
    ys = cat("ys")[:, None, :]
    poolp = cat("poolp")[None]
    kvp = [cat("kvp%d" % w).reshape(16, min(w, SEQ), 2, 4, 64)[None] for (w, _) in GROUPS]
    pools = cat("pools")[None]
    kvs = [cat("kvs%d" % w).reshape(32, w, 2, 4, 64)[None] for (w, _) in GROUPS]
    return (y, ys, poolp, kvp[0], kvp[1], kvp[2], pools, kvs[0], kvs[1], kvs[2])


def kernel(**inputs):
    nc = build(CFG)
    in_maps = make_in_maps(inputs)
    res = run_bass_kernel_spmd(nc, in_maps, core_ids=list(range(8)))
    return tuple(np.ascontiguousarray(a, dtype=np.float32) for a in assemble(res.results))
```

```python
import contextlib
import numpy as np
import concourse.bass as bass
import concourse.mybir as mybir
from concourse.bass_utils import run_bass_kernel_spmd

F32 = mybir.dt.float32
BF16 = mybir.dt.bfloat16
AF = mybir.ActivationFunctionType
ALU = mybir.AluOpType

D = 1024
FF = 2816
NJ = FF // 128
NW8 = 8
T = 512
SEQ = 2048
NSEQ = 2
NTILE = SEQ // T
NS = 4
PAST = 16384
EPS = 1e-6
GROUPS = ((128, 1), (512, 4), (2048, 16))
OFF_U = 0
OFF_Q = 512
OFF_K = 512 + 768
OFF_V = 512 + 1536
OFF_GP = OFF_V + 768
OFF_GA = OFF_GP + 1024
CFG = {"nseq": NSEQ, "ntile": NTILE, "sample": True}


class StopBuild(Exception):
    pass


class Res:
    __slots__ = ("w", "r", "name")

    def __init__(self, name=""):
        self.w = None
        self.r = {}
        self.name = name


class DmaSem:
    def __init__(self, K, name):
        self.sem = K.es.enter_context(K.nc.semaphore(name))
        self.total = 0


class EngW:
    def __init__(self, K, eng, name, is_pe=False):
        self.K = K
        self.eng = eng
        self.name = name
        self.sem = K.es.enter_context(K.nc.semaphore("sem_" + name))
        self.cnt = 0
        self.known = {}
        self.is_pe = is_pe

    def wait_tok(self, tok):
        sem, val = tok
        if sem is self.sem:
            if self.is_pe:
                return
            assert val <= self.cnt, (self.name, val, self.cnt)
        if self.known.get(id(sem), 0) >= val:
            return
        self.eng.wait_ge(sem, val)
        self.known[id(sem)] = val

    def wait_deps(self, reads, writes):
        for r in reads:
            if r.w is not None:
                self.wait_tok(r.w)
        for w in writes:
            if w.w is not None:
                self.wait_tok(w.w)
            for tok in w.r.values():
                self.wait_tok(tok)

    @staticmethod
    def commit(tok, reads, writes):
        for r in reads:
            old = r.r.get(id(tok[0]))
            if old is None or old[1] < tok[1]:
                r.r[id(tok[0])] = tok
        for w in writes:
            w.w = tok
            w.r = {}

    def op(self, fn, reads=(), writes=(), inc=True):
        self.nops = getattr(self, "nops", 0) + 1
        self.wait_deps(reads, writes)
        ins = fn()
        if inc:
            ins.then_inc(self.sem, 1)
            self.cnt += 1
            tok = (self.sem, self.cnt)
        else:
            tok = (self.sem, self.cnt + 1)
        self.commit(tok, reads, writes)
        return tok

    def dma(self, out, in_, dsem, reads=(), writes=(), **kw):
        self.wait_deps(reads, writes)
        self.eng.dma_start(out=out, in_=in_, **kw).then_inc(dsem.sem, 16)
        dsem.total += 16
        tok = (dsem.sem, dsem.total)
        self.commit(tok, reads, writes)
        return tok


class Buf:
    def __init__(self, K, name, shape, dtype, nres=1):
        K.uid = getattr(K, "uid", 0) + 1
        self.t = K.es_cur.enter_context(K.nc.sbuf_tensor("sb_%s_%d" % (name, K.uid), shape, dtype))
        self.r = [Res(name + str(i)) for i in range(nres)]

    def __getitem__(self, idx):
        return self.t[idx]


class Seg:
    def __init__(self, n, xT, hT, rstd, rope):
        self.n, self.xT, self.hT, self.rstd, self.rope = n, xT, hT, rstd, rope


class View:
    def __init__(self, buf, fn):
        self.r = buf.r
        self.fn = fn

    def __getitem__(self, idx):
        return self.fn()[idx]


class Ring:
    def __init__(self, K, name, shape, dtype, n, sems=None):
        self.slots = [Buf(K, "%s%d" % (name, i), shape, dtype) for i in range(n)]
        self.sems = sems if sems is not None else [DmaSem(K, "ds_%s%d" % (name, i)) for i in range(n)]
        self.n = n
        self.i = 0

    def next(self):
        s = self.i % self.n
        self.i += 1
        return self.slots[s], self.sems[s]


class Kern:
    def __init__(self):
        self.nc = bass.Bass("TRN2", target_bir_lowering=False)
        self.es = contextlib.ExitStack()
        self.es_cur = self.es

    def dram_in(self, name, shape):
        return self.nc.dram_tensor(name, list(shape), F32, kind="ExternalInput").ap()

    def dram_out(self, name, shape):
        return self.nc.dram_tensor(name, list(shape), F32, kind="ExternalOutput").ap()

    def setup(self):
        nc = self.nc
        self.pe = EngW(self, nc.tensor, "pe", is_pe=True)
        self.act = EngW(self, nc.scalar, "act")
        self.dve = EngW(self, nc.vector, "dve")
        self.pool = EngW(self, nc.gpsimd, "pool")
        self.sp = EngW(self, nc.sync, "sp")
        self.ps = [self.es.enter_context(nc.psum_tensor("ps%d" % i, [128, 512], F32)) for i in range(8)]
        self.psr = [Res("ps%d" % i) for i in range(8)]
        self.ps_free = list(range(8))
        self.ps_i = 0
        self.all_dsems = []
        self.arena_dsems = []
        self.evac_i = 0

    def dsem(self, name):
        d = DmaSem(self, name)
        self.all_dsems.append(d)
        return d

    def bank(self):
        b = self.ps_free[self.ps_i % len(self.ps_free)]
        self.ps_i += 1
        return b

    def mmg(self, bank, items, reads, first_start=True, stop_last=True, write=True):
        n = len(items)
        for i, (o, l, r, kw) in enumerate(items):
            st = kw.pop("start", (i == 0) and first_start)
            sp_ = kw.pop("stop", (i == n - 1) and stop_last)
            last = (i == n - 1)
            rd_i = kw.pop("rd", None)
            if kw.pop("is_transpose", False):
                fn = lambda o=o, l=l, r=r: self.nc.tensor.transpose(o, l, r)
            else:
                fn = lambda o=o, l=l, r=r, st=st, sp_=sp_, kw=kw: self.nc.tensor.matmul(o, l, r, start=st, stop=sp_, **kw)
            rds = (list(reads) if i == 0 else []) + (list(rd_i) if rd_i else [])
            self.pe.op(fn, reads=rds, writes=[self.psr[bank]] if (i == 0 and write) else (), inc=last)
        tok = (self.pe.sem, self.pe.cnt)
        EngW.commit(tok, reads, [self.psr[bank]])
        return tok

    def evac_eng(self):
        self.evac_i += 1
        return self.act if (self.evac_i % 2) else self.dve

    def copy(self, eng, out, in_, reads, writes):
        if eng is self.act:
            return eng.op(lambda: self.nc.scalar.copy(out=out, in_=in_), reads, writes)
        return eng.op(lambda: self.nc.vector.tensor_copy(out=out, in_=in_), reads, writes)

    def barrier_tokens(self):
        toks = []
        for e in (self.pe, self.act, self.dve, self.pool):
            if e.cnt > 0:
                toks.append((e.sem, e.cnt))
        for d in self.arena_dsems:
            if d.total > 0:
                toks.append((d.sem, d.total))
        return toks

    def barrier(self):
        toks = self.barrier_tokens()
        for e in (self.pe, self.act, self.dve):
            for tk in toks:
                if tk[0] is not e.sem:
                    e.wait_tok(tk)
        self.arena_toks = toks


def build(cfg=CFG):
    K = Kern()
    nc = K.nc
    nseq, ntile, do_sample = cfg["nseq"], cfg["ntile"], cfg["sample"]
    x_d = K.dram_in("x", [NSEQ, SEQ, D])
    xs_d = K.dram_in("xs", [NS, D])
    spool_d = K.dram_in("spool", [NS, 15, 512])
    cache_d = [K.dram_in("c%d" % w, [NS, w, 2, 256]) for (w, _) in GROUPS]
    gains_d = K.dram_in("gains", [4, D])
    pscale_d = K.dram_in("pscale", [512])
    wg_d = [K.dram_in("wg%d" % i, [D, FF]) for i in (1, 2)]
    wu_d = [K.dram_in("wu%d" % i, [D, FF]) for i in (1, 2)]
    wd_d = [K.dram_in("wd%d" % i, [FF, D]) for i in (1, 2)]
    win_d = K.dram_in("win", [D, 4864])
    wgrp_d = K.dram_in("wgrp", [4, 128, 128])
    wpb_d = K.dram_in("wpb", [512, D])
    wab_d = K.dram_in("wab", [256, D])
    wo_d = K.dram_in("wo", [D, D])
    ident_d = K.dram_in("ident", [128, 128])
    mask01_d = K.dram_in("mask01", [128, 2, 4, 128])
    mask2_d = K.dram_in("mask2", [128, 16, 32])
    rope_d = K.dram_in("rope", [5, 2, 128, 512])
    rcnt_d = K.dram_in("rcnt", [128, 4, 16])

    y_d = K.dram_out("y", [NSEQ, SEQ, D])
    ys_d = K.dram_out("ys", [NS, D])
    poolp_d = K.dram_out("poolp", [NSEQ, 15, 512])
    kvp_d = [K.dram_out("kvp%d" % w, [NSEQ, min(w, SEQ), 2, 256]) for (w, _) in GROUPS]
    pools_d = K.dram_out("pools", [NS, 15, 512])
    kvs_d = [K.dram_out("kvs%d" % w, [NS, w, 2, 256]) for (w, _) in GROUPS]

    with K.es:
        K.setup()
        pe, act, dve, pool, sp = K.pe, K.act, K.dve, K.pool, K.sp
        ps, psr = K.ps, K.psr

        ident = Buf(K, "ident", [128, 128], F32)
        mask01 = Buf(K, "mask01", [128, 2, 4, 128], BF16)
        mask2 = Buf(K, "mask2", [128, 16, 32], BF16)
        rcnt = Buf(K, "rcnt", [128, 4, 16], F32)
        gains = Buf(K, "gains", [128, 4, 8], F32)
        pscale = Buf(K, "pscale", [128, 4], F32)
        wgrp = Buf(K, "wgrp", [128, 4, 128], BF16)
        ones = Buf(K, "ones", [128, 128], BF16)
        dummy = Buf(K, "dummy", [128, 4], F32)
        meanm = Buf(K, "meanm", [128, 128], BF16)
        zeros = Buf(K, "zeros", [1, 512], BF16)
        xin = Buf(K, "xin", [128, 4, D], F32)
        xT = Buf(K, "xT", [128, 8, T], F32, nres=8)
        hT = Buf(K, "hT", [128, 8, T], BF16, nres=8)
        rstd = Buf(K, "rstd", [128, T], F32)
        rope = Buf(K, "rope", [128, 2, T], F32)
        ucarry = Buf(K, "ucarry", [128, 4, 15], F32)
        kT0 = Buf(K, "kT0", [128, 2, 5 * 128], BF16)
        kT1 = Buf(K, "kT1", [128, 2, 2 * 512], BF16)
        kT2 = Buf(K, "kT2", [128, 2, 16 * 128], BF16)
        V0 = Buf(K, "V0", [128, 5, 256], BF16)
        V1 = Buf(K, "V1", [128, 8, 256], BF16)
        V2 = Buf(K, "V2", [128, 16, 256], BF16)
        xTs = Buf(K, "xTs", [128, 8, NS], F32, nres=8)
        hTs = Buf(K, "hTs", [128, 8, NS], BF16, nres=8)
        rstds = Buf(K, "rstds", [128, NS], F32)
        rope_s = Buf(K, "rope_s", [128, 2, NS], F32)
        SP = Seg(T, xT, hT, rstd, rope)
        SS = Seg(NS, xTs, hTs, rstds, rope_s)
        w8 = Ring(K, "w8", [128, 8, 256], BF16, NW8)
        w8_sem3 = [[K.dsem("ds_w8%s%d" % (k, i)) for k in "pwh"] for i in range(NW8)]
        wd_sem3 = [[K.dsem("ds_wd%s%d" % (k, i)) for k in "pwh"] for i in range(3)]
        wv_sem3 = [K.dsem("ds_wv%s" % k) for k in "pwh"]
        wd_sems = [wd_sem3[i][0] for i in range(3)]
        st_sems = [{k: K.dsem("ds_%s%d" % (k, i)) for k in ("kv", "kv2", "kt", "pt", "y", "ys")} for i in range(2)]
        ds_sp = K.dsem("ds_sp")
        K.arena_dsems = [d for tr in wd_sem3 for d in tr] + wv_sem3 + [st_sems[i][k] for i in range(2) for k in ("kv", "kv2", "kt", "pt")]
        K.pass_idx = 0
        wscr8 = nc.dram_tensor("wscr8", [72, 128, 2048], BF16, kind="Internal").ap()
        wscrd = nc.dram_tensor("wscrd", [8, 128, NJ * 256], BF16, kind="Internal").ap()
        wscrv = nc.dram_tensor("wscrv", [128, 8 * 768], BF16, kind="Internal").ap()
        scr8_res = [Res("scr8_%d" % i) for i in range(72)]
        scrd_res = [Res("scrd_%d" % i) for i in range(8)]
        scrv_res = Res("scrv")

        def st_dma(kind, out, in_, reads):
            if K.pass_idx == 0:
                return sp.dma(out, in_, st_sems[0][kind], reads=reads)
            return pool.dma(out, in_, st_sems[1][kind], reads=reads)

        def wload(first_pass, slot, slot_ap, src_f32, scr_ap, scr_res, sem3, arena_bound):
            if first_pass:
                if arena_bound:
                    for tk in K.arena_toks:
                        pool.wait_tok(tk)
                pool.dma(slot_ap, src_f32, sem3[0], writes=slot.r)
                sp.dma(scr_ap, slot_ap, sem3[1], reads=slot.r, writes=[scr_res])
            else:
                if arena_bound:
                    for tk in K.arena_toks:
                        sp.wait_tok(tk)
                sp.dma(slot_ap, scr_ap, sem3[2], reads=[scr_res], writes=slot.r)
        ds_const = K.dsem("ds_const")
        ds_constp = K.dsem("ds_constp")
        ds_x = K.dsem("ds_x")
        ds_y = K.dsem("ds_y")
        ds_rope = K.dsem("ds_rope")
        ds_misc = K.dsem("ds_misc")
        K.arena_toks = []

        sp.dma(ident[:], ident_d[:, :], ds_const, writes=ident.r)
        sp.dma(rcnt[:], rcnt_d[:, :, :], ds_const, writes=rcnt.r)
        sp.dma(pscale[:], pscale_d.rearrange("(c p) -> p c", p=128), ds_const, writes=pscale.r, allow_slow_non_contiguous=True)
        for i in range(4):
            sp.dma(gains[:, i, :], gains_d[i, :].rearrange("(c p) -> p c", p=128), ds_const, writes=gains.r,
                   allow_slow_non_contiguous=True)
        pool.dma(mask01[:], mask01_d[:, :, :, :], ds_constp, writes=mask01.r)
        pool.dma(mask2[:], mask2_d[:, :, :], ds_constp, writes=mask2.r)
        pool.dma(wgrp[:], wgrp_d.rearrange("g c d -> c g d"), ds_constp, writes=wgrp.r)
        for bufc in (ident, rcnt, pscale, gains):
            bufc.r[0].w = (ds_const.sem, ds_const.total)
        for bufc in (mask01, mask2, wgrp):
            bufc.r[0].w = (ds_constp.sem, ds_constp.total)
        dve.op(lambda: nc.vector.memset(ones[:], 1.0), writes=ones.r)
        dve.op(lambda: nc.vector.memset(dummy[:], 1.0), writes=dummy.r)
        act.op(lambda: nc.scalar.copy(out=dummy[:, 3:4], in_=dummy[:, 2:3]), reads=dummy.r)
        dve.op(lambda: nc.vector.memset(meanm[:], 1.0 / D), writes=meanm.r)
        dve.op(lambda: nc.vector.memset(zeros[:], 0.0), writes=zeros.r)

        w8_plan = []
        w8_state = {"issued": 0, "loaded": []}

        def w8_src(w_ap, c0, ncols, krows=D):
            kc = krows // 128
            return (w_ap.rearrange("(kc p) n -> p kc n", p=128)[:, :, c0:c0 + ncols], kc, ncols)

        def w8_issue_upto(n):
            while w8_state["issued"] < min(n, len(w8_plan)):
                ii = w8_state["issued"]
                src, kc, ncols = w8_plan[ii]
                pi_ = ii % 72
                si = w8.i % w8.n
                slot, _ = w8.next()
                wload(ii < 72, slot, slot[:, 0:kc, 0:ncols], src,
                      wscr8[pi_, :, 0:kc * ncols].rearrange("p (k n) -> p k n", n=ncols), scr8_res[pi_], w8_sem3[si], False)
                w8_state["loaded"].append(slot)
                w8_state["issued"] += 1

        w8_cons = {"i": 0}

        def w8_get():
            i = w8_cons["i"]
            w8_issue_upto(i + 1)
            slot = w8_state["loaded"][i]
            w8_cons["i"] += 1
            return slot

        def w8_done():
            w8_issue_upto(w8_cons["i"] + w8.n - 1)

        def plan_pass():
            for f in (0,):
                for jp in range(NJ // 2):
                    w8_plan.append(w8_src(wg_d[0], jp * 256, 256))
                    w8_plan.append(w8_src(wu_d[0], jp * 256, 256))
            for c0 in range(0, 512 + 1536, 256):
                w8_plan.append(w8_src(win_d, c0, 256))
            for mp in range(4):
                w8_plan.append(w8_src(win_d, OFF_GP + mp * 256, 256))
                w8_plan.append(w8_src(win_d, OFF_GA + mp * 256, 256))
                w8_plan.append(w8_src(wpb_d, mp * 256, 256, krows=512))
                w8_plan.append(w8_src(wab_d, mp * 256, 256, krows=256))
            for mp in range(4):
                w8_plan.append(w8_src(wo_d, mp * 256, 256))
            for jp in range(NJ // 2):
                w8_plan.append(w8_src(wg_d[1], jp * 256, 256))
                w8_plan.append(w8_src(wu_d[1], jp * 256, 256))

        n_pass = max(nseq * ntile, 1 if do_sample else 0)
        K.n_pass = n_pass
        for _ in range(n_pass):
            plan_pass()

        def norm(gi, seg, out_f32=None, next_func=None):
            n, xT_, hT_, rstd_ = seg.n, seg.xT, seg.hT, seg.rstd
            act.op(lambda: nc.scalar.activation(out=dummy[:, 0:1], in_=dummy[:, 2:3], func=AF.Ln))
            for c in range(8):
                if c % 2 == 0:
                    act.op(lambda c=c: nc.scalar.activation(out=hT_[:, c, 0:n], in_=xT_[:, c, 0:n], func=AF.Square),
                           reads=[xT_.r[c]], writes=[hT_.r[c]])
                else:
                    dve.op(lambda c=c: nc.vector.tensor_tensor(out=hT_[:, c, 0:n], in0=xT_[:, c, 0:n], in1=xT_[:, c, 0:n], op=ALU.mult),
                           reads=[xT_.r[c]], writes=[hT_.r[c]])
            b = K.bank()
            K.mmg(b, [(ps[b][:, 0:n], meanm[:], hT_[:, c, 0:n], {"rd": [hT_.r[c]]}) for c in range(8)], reads=meanm.r)
            act.op(lambda: nc.scalar.activation(out=rstd_[:, 0:n], in_=ps[b][:, 0:n], func=AF.Ln, bias=EPS, scale=1.0),
                   reads=[psr[b]], writes=rstd_.r)
            act.op(lambda: nc.scalar.activation(out=rstd_[:, 0:n], in_=rstd_[:, 0:n], func=AF.Exp, scale=-0.5), reads=rstd_.r,
                   writes=rstd_.r)
            if next_func is not None:
                act.op(lambda: nc.scalar.activation(out=dummy[:, 1:2], in_=dummy[:, 2:3], func=next_func))
            for c in range(8):
                if out_f32 is None:
                    dve.op(lambda c=c: nc.vector.scalar_tensor_tensor(
                        out=hT_[:, c, 0:n], in0=xT_[:, c, 0:n], scalar=gains[:, gi, c:c + 1], in1=rstd_[:, 0:n],
                        op0=ALU.mult, op1=ALU.mult), reads=[xT_.r[c]] + rstd_.r + gains.r, writes=[hT_.r[c]])
                else:
                    dve.op(lambda c=c: nc.vector.scalar_tensor_tensor(
                        out=xT_[:, c, 0:n], in0=xT_[:, c, 0:n], scalar=gains[:, gi, c:c + 1], in1=rstd_[:, 0:n],
                        op0=ALU.mult, op1=ALU.mult), reads=[xT_.r[c]] + rstd_.r + gains.r + hT_.r, writes=[xT_.r[c]])

        def ffn(f, segs):
            wdr = K.wdr
            wd_slots = []

            def wd_issue(mp):
                si = wdr.i % wdr.n
                slot, _ = wdr.next()
                wload(K.pass_idx == 0, slot, slot[:], wd_d[f].rearrange("(j p) n -> p j n", p=128)[:, :, mp * 256:(mp + 1) * 256],
                      wscrd[4 * f + mp, :, :].rearrange("p (j n) -> p j n", n=256), scrd_res[4 * f + mp], wd_sem3[si], True)
                wd_slots.append(slot)

            for mp in range(3):
                wd_issue(mp)
            for jp in range(NJ // 2):
                gs = w8_get()
                us = w8_get()
                for jj in range(2):
                    j = 2 * jp + jj
                    for seg in segs:
                        n, hT_, actT, stmp = seg.n, seg.hT, seg.actT, seg.stmp
                        bg = K.bank()
                        K.mmg(bg, [(ps[bg][:, 0:n], gs[:, kc, jj * 128:(jj + 1) * 128], hT_[:, kc, 0:n], {"rd": [hT_.r[kc]]}) for kc in range(8)],
                              reads=gs.r)
                        bu = K.bank()
                        K.mmg(bu, [(ps[bu][:, 0:n], us[:, kc, jj * 128:(jj + 1) * 128], hT_[:, kc, 0:n], {"rd": [hT_.r[kc]]}) for kc in range(8)],
                              reads=us.r)
                        st = stmp[j % 2]
                        act.op(lambda st=st, bg=bg, n=n: nc.scalar.activation(out=st[:, 0:n], in_=ps[bg][:, 0:n], func=AF.Silu),
                               reads=[psr[bg]], writes=st.r)
                        dve.op(lambda st=st, bu=bu, j=j, n=n, actT=actT: nc.vector.tensor_tensor(
                            out=actT[:, j, 0:n], in0=st[:, 0:n], in1=ps[bu][:, 0:n], op=ALU.mult),
                            reads=st.r + [psr[bu]], writes=[actT.r[j]])
                w8_done()
            JS = NJ - 4
            for mp in range(4):
                slot = wd_slots[mp]
                if mp == 0:
                    banks = {}
                    for mm_ in range(2):
                        for si, seg in enumerate(segs):
                            n, actT = seg.n, seg.actT
                            b = K.bank()
                            banks[(mm_, si)] = b
                            K.mmg(b, [(ps[b][:, 0:n], slot[:, j, mm_ * 128:(mm_ + 1) * 128], actT[:, j, 0:n], {"rd": [actT.r[j]]})
                                      for j in range(JS)], reads=slot.r, stop_last=False)
                    for mm_ in range(2):
                        m = mm_
                        for si, seg in enumerate(segs):
                            n, xT_, actT = seg.n, seg.xT, seg.actT
                            b = banks[(mm_, si)]
                            K.mmg(b, [(ps[b][:, 0:n], slot[:, j, mm_ * 128:(mm_ + 1) * 128], actT[:, j, 0:n], {"rd": [actT.r[j]]})
                                      for j in range(JS, NJ)], reads=slot.r, first_start=False)
                            dve.op(lambda b=b, m=m, n=n, xT_=xT_: nc.vector.scalar_tensor_tensor(
                                out=xT_[:, m, 0:n], in0=ps[b][:, 0:n], scalar=0.5, in1=xT_[:, m, 0:n], op0=ALU.mult, op1=ALU.add),
                                reads=[psr[b], xT_.r[m]], writes=[xT_.r[m]])
                    wd_issue(3)
                    continue
                for mm_ in range(2):
                    m = 2 * mp + mm_
                    for seg in segs:
                        n, xT_, actT = seg.n, seg.xT, seg.actT
                        b = K.bank()
                        K.mmg(b, [(ps[b][:, 0:n], slot[:, j, mm_ * 128:(mm_ + 1) * 128], actT[:, j, 0:n], {"rd": [actT.r[j]]}) for j in range(NJ)],
                              reads=slot.r)
                        dve.op(lambda b=b, m=m, n=n, xT_=xT_: nc.vector.scalar_tensor_tensor(
                            out=xT_[:, m, 0:n], in0=ps[b][:, 0:n], scalar=0.5, in1=xT_[:, m, 0:n], op0=ALU.mult, op1=ALU.add),
                            reads=[psr[b], xT_.r[m]], writes=[xT_.r[m]])

        @contextlib.contextmanager
        def arena():
            if not getattr(K, "exit_done", False):
                K.barrier()
            st = contextlib.ExitStack()
            old = K.es_cur
            K.es_cur = st
            try:
                with st:
                    try:
                        yield
                    finally:
                        K.barrier()
                        K.exit_done = True
            finally:
                K.es_cur = old

        K.last_y_tok = None

        def ffn_stage(f, gi, segs, tail=None):
            for seg in segs[::-1]:
                norm(gi, seg, next_func=AF.Silu)
            with arena():
                ystage = Buf(K, "ystage", [128, 4, D], F32) if tail is not None else None
                for si, seg in enumerate(segs):
                    seg.actT = Buf(K, "actT", [128, NJ, seg.n], BF16, nres=NJ)
                    if si == 0 and tail is None and K.last_y_tok is not None:
                        for r_ in seg.actT.r:
                            r_.r[id(K.last_y_tok[0])] = K.last_y_tok
                    seg.stmp = [Buf(K, "stmp%d" % i, [128, seg.n], F32) for i in range(2)]
                K.wdr = Ring(K, "wd", [128, NJ, 256], BF16, 3, sems=wd_sems)
                ffn(f, segs)
                if tail is not None:
                    K.last_y_tok = tail(ystage)
            issue_copy(1)

        def proj_chunks(nchunks, segs, consumes):
            for cp in range(nchunks // 2):
                ws = w8_get()
                for cc in range(2):
                    for seg, consume in zip(segs, consumes):
                        n, hT_ = seg.n, seg.hT
                        b = K.bank()
                        K.mmg(b, [(ps[b][:, 0:n], ws[:, kc, cc * 128:(cc + 1) * 128], hT_[:, kc, 0:n], {"rd": [hT_.r[kc]]}) for kc in range(8)],
                              reads=ws.r)
                        consume(2 * cp + cc, b)
                w8_done()

        def rope_pair(bA, bB, n, outA, outB, rtmp, resA, resB, view=lambda a: a, rope=rope):
            cos, sin = rope[:, 0, 0:n], rope[:, 1, 0:n]
            t = rtmp
            dve.op(lambda: nc.vector.tensor_tensor(out=t[0][:, 0:n], in0=ps[bA][:, 0:n], in1=cos, op=ALU.mult),
                   reads=[psr[bA]] + rope.r, writes=t[0].r)
            dve.op(lambda: nc.vector.tensor_tensor(out=t[1][:, 0:n], in0=ps[bB][:, 0:n], in1=sin, op=ALU.mult),
                   reads=[psr[bB]] + rope.r, writes=t[1].r)
            dve.op(lambda: nc.vector.tensor_tensor(out=outA, in0=view(t[0][:, 0:n]), in1=view(t[1][:, 0:n]), op=ALU.subtract),
                   reads=t[0].r + t[1].r, writes=resA)
            dve.op(lambda: nc.vector.tensor_tensor(out=t[2][:, 0:n], in0=ps[bA][:, 0:n], in1=sin, op=ALU.mult),
                   reads=[psr[bA]] + rope.r, writes=t[2].r)
            dve.op(lambda: nc.vector.tensor_tensor(out=t[3][:, 0:n], in0=ps[bB][:, 0:n], in1=cos, op=ALU.mult),
                   reads=[psr[bB]] + rope.r, writes=t[3].r)
            dve.op(lambda: nc.vector.tensor_tensor(out=outB, in0=view(t[2][:, 0:n]), in1=view(t[3][:, 0:n]), op=ALU.add),
                   reads=t[2].r + t[3].r, writes=resB)

        def gates_and_wo(items):
            for mp in range(4):
                wgp = w8_get()
                wga = w8_get()
                wpb = w8_get()
                wab = w8_get()
                for mm_ in range(2):
                    m = 2 * mp + mm_
                    cs = slice(mm_ * 128, (mm_ + 1) * 128)
                    for (seg, poolT, attnT, mergedT, gt) in items:
                        n, hT_ = seg.n, seg.hT
                        b1 = K.bank()
                        K.mmg(b1, [(ps[b1][:, 0:n], wgp[:, kc, cs], hT_[:, kc, 0:n], {}) for kc in range(8)], reads=hT_.r + wgp.r)
                        b2 = K.bank()
                        K.mmg(b2, [(ps[b2][:, 0:n], wga[:, kc, cs], hT_[:, kc, 0:n], {}) for kc in range(8)], reads=hT_.r + wga.r)
                        b3 = K.bank()
                        K.mmg(b3, [(ps[b3][:, 0:n], wpb[:, kc, cs], poolT[:, kc, 0:n], {}) for kc in range(4)], reads=poolT.r + wpb.r)
                        b4 = K.bank()
                        K.mmg(b4, [(ps[b4][:, 0:n], wab[:, kc, cs], attnT[:, kc, 0:n], {}) for kc in range(2)], reads=attnT.r + wab.r)
                        act.op(lambda b1=b1, n=n, gt=gt: nc.scalar.activation(out=gt[0][:, 0:n], in_=ps[b1][:, 0:n], func=AF.Sigmoid),
                               reads=[psr[b1]], writes=gt[0].r)
                        act.op(lambda b2=b2, n=n, gt=gt: nc.scalar.activation(out=gt[1][:, 0:n], in_=ps[b2][:, 0:n], func=AF.Sigmoid),
                               reads=[psr[b2]], writes=gt[1].r)
                        dve.op(lambda b3=b3, n=n, gt=gt: nc.vector.tensor_tensor(out=gt[0][:, 0:n], in0=gt[0][:, 0:n], in1=ps[b3][:, 0:n],
                                                                                 op=ALU.mult), reads=gt[0].r + [psr[b3]], writes=gt[0].r)
                        dve.op(lambda b4=b4, n=n, gt=gt: nc.vector.tensor_tensor(out=gt[1][:, 0:n], in0=gt[1][:, 0:n], in1=ps[b4][:, 0:n],
                                                                                 op=ALU.mult), reads=gt[1].r + [psr[b4]], writes=gt[1].r)
                        dve.op(lambda m=m, n=n, gt=gt, mergedT=mergedT: nc.vector.tensor_tensor(
                            out=mergedT[:, m, 0:n], in0=gt[0][:, 0:n], in1=gt[1][:, 0:n], op=ALU.add),
                            reads=gt[0].r + gt[1].r, writes=[mergedT.r[m]])
                w8_done()
            for mp in range(4):
                wo = w8_get()
                for mm_ in range(2):
                    m = 2 * mp + mm_
                    for (seg, poolT, attnT, mergedT, gt) in items:
                        n, xT_ = seg.n, seg.xT
                        b = K.bank()
                        K.mmg(b, [(ps[b][:, 0:n], wo[:, kc, mm_ * 128:(mm_ + 1) * 128], mergedT[:, kc, 0:n], {"rd": [mergedT.r[kc]]}) for kc in range(8)],
                              reads=wo.r)
                        dve.op(lambda b=b, m=m, n=n, xT_=xT_: nc.vector.tensor_tensor(out=xT_[:, m, 0:n], in0=ps[b][:, 0:n],
                                                                                      in1=xT_[:, m, 0:n], op=ALU.add),
                               reads=[psr[b], xT_.r[m]], writes=[xT_.r[m]])
                w8_done()

        K.marks = []

        def mark(name):
            K.marks.append((name, getattr(pe, "nops", 0)))
        K.mark = mark

        def dbg(level):
            return cfg.get("stop", 99) <= level

        def tp(kbase, obase):
            if kbase == 96 or obase == 96:
                return {"tile_position": (kbase, obase)}
            return {}

        def mixer_prompt(s, t, shared):
            n = T
            norm(1, SP)
            with arena():
                uext = Buf(K, "uext", [128, 4, 15 + T], F32, nres=4)
                wtmp = [Buf(K, "wtmp%d" % i, [128, 15 + T], F32) for i in range(2)]
                pT = Buf(K, "pT", [128, 4, T], BF16, nres=4)
                poolT = Buf(K, "poolT", [128, 4, T], BF16)
                qT = Buf(K, "qT", [128, 6, T], BF16, nres=6)
                rtmp = [Buf(K, "rtmp%d" % i, [128, T], F32) for i in range(4)]
                kf = Buf(K, "kf", [128, 2, T], F32, nres=2)
                ktok = Buf(K, "ktok", [128, 4, 256], F32)
                vtok = Buf(K, "vtok", [128, 4, 256], F32)
                vtok2 = Buf(K, "vtok2", [128, 4, 256], F32)
                pbuf = [Buf(K, "pbuf%d" % i, [128, T], BF16) for i in range(3)]
                attnT = Buf(K, "attnT", [128, 2, T], BF16)
                rden = Buf(K, "rden", [128, 64], F32)
                mergedT = Buf(K, "mergedT", [128, 8, T], BF16, nres=8)
                gt = [Buf(K, "gt%d" % i, [128, T], F32) for i in range(2)]
                wv = Buf(K, "wv", [128, 8, 768], BF16)
                ptok = Buf(K, "ptok", [16, 512], F32)
                hT16 = Buf(K, "hT16", [128, 8, T], BF16)
                shared.update(dict(wv=wv, ptok=ptok, ktok=ktok, rtmp=rtmp, hT16=hT16, mergedT=mergedT, pbuf=pbuf, vtok=vtok,
                                   vtok2=vtok2))
                hT4 = mergedT
                for kc in range(8):
                    K.copy(K.evac_eng(), hT4[:, kc, :].rearrange("p (r i) -> p i r", r=4),
                           hT[:, kc, :].rearrange("p (i r) -> p i r", r=4), hT.r, hT4.r)
                    K.copy(K.evac_eng(), hT16[:, kc, :].rearrange("p (r i) -> p i r", r=16),
                           hT[:, kc, :].rearrange("p (i r) -> p i r", r=16), hT.r, hT16.r)
                wload(K.pass_idx == 0, wv, wv[:], win_d.rearrange("(kc p) n -> p kc n", p=128)[:, :, OFF_V:OFF_V + 768],
                      wscrv.rearrange("p (k n) -> p k n", n=768), scrv_res, wv_sem3, True)
                sp.dma(rope[:], rope_d[t].rearrange("a p n -> p a n"), ds_rope, writes=rope.r)
                if t == 0:
                    dve.op(lambda: nc.vector.memset(ucarry[:], 0.0), writes=ucarry.r)
                for g in range(4):
                    dve.op(lambda g=g: nc.vector.tensor_copy(out=uext[:, g, 0:15], in_=ucarry[:, g, :]), reads=ucarry.r,
                           writes=[uext.r[g]])

                def cons_u(g, b):
                    act.op(lambda: nc.scalar.copy(out=uext[:, g, 15:15 + n], in_=ps[b][:, 0:n]),
                           reads=[psr[b]], writes=[uext.r[g]])
                yield ("u", cons_u)
                for g in range(4):
                    dve.op(lambda g=g: nc.vector.tensor_copy(out=ucarry[:, g, :], in_=uext[:, g, T:T + 15]), reads=[uext.r[g]],
                           writes=ucarry.r)
                if t == NTILE - 1:
                    b = K.bank()
                    K.mmg(b, [(ps[b][0:15, g * 128:(g + 1) * 128], uext[:, g, T:T + 15], ident[:], {"start": True, "stop": True,
                                                                                                      "is_transpose": True})
                              for g in range(4)], reads=uext.r + ident.r)
                    act.op(lambda: nc.scalar.copy(out=ptok[0:15, :], in_=ps[b][0:15, :]), reads=[psr[b]],
                           writes=ptok.r)
                    st_dma('pt', poolp_d[s, :, :], ptok[0:15, :], ptok.r)
                mark("m.qk")
                def qview(g, ab):
                    if g == 0:
                        return qT[:, ab, :]
                    dil = GROUPS[g][1]
                    return qT[:, 2 * g + ab, :].rearrange("p (r i) -> p i r", r=dil)

                def k_dst(g, ab):
                    if g == 0:
                        return [(kT0[:, ab, ((4 * t + b_) % 5) * 128:((4 * t + b_) % 5 + 1) * 128], slice(b_ * 128, (b_ + 1) * 128))
                                for b_ in range(4)]
                    if g == 1:
                        base = (t % 2) * 512
                        return [(kT1[:, ab, base:base + 512].rearrange("p (r i) -> p i r", r=4), slice(0, T))]
                    v = kT2[:, ab, :].rearrange("p (r m) -> p m r", r=16)
                    return [(v[:, 32 * t:32 * t + 32, :], slice(0, T))]

                pend = {}

                def emit_kout(g):
                    need = (g == 2) or (t == NTILE - 1)
                    if not need:
                        return
                    for b_ in range(4):
                        b = K.bank()
                        K.mmg(b, [(ps[b][:, ab * 128:(ab + 1) * 128], kf[:, ab, b_ * 128:(b_ + 1) * 128], ident[:],
                                   {"start": True, "stop": True, "is_transpose": True}) for ab in range(2)], reads=kf.r + ident.r)
                        act.op(lambda b=b, b_=b_: nc.scalar.copy(
                            out=ktok[:, b_, :].rearrange("p (h ab i) -> p ab h i", ab=2, i=32),
                            in_=ps[b][:, 0:256].rearrange("p (ab h i) -> p ab h i", ab=2, i=32)),
                            reads=[psr[b]], writes=ktok.r)
                    W = GROUPS[g][0]
                    keep = min(W, SEQ)
                    r0 = t * T - (SEQ - keep)
                    if g == 0:
                        st_dma('kt', kvp_d[0][s, :, 0, :], ktok[:, 3, :], ktok.r)
                    else:
                        st_dma('kt', kvp_d[g][s, r0:r0 + T, 0, :].rearrange("(b p) f -> p b f", p=128), ktok[:, :, :], ktok.r)

                def cons_qk(idx, b):
                    isk = idx >= 6
                    g = (idx % 6) // 2
                    ab = idx % 2
                    if ab == 0:
                        pend["A"] = b
                        return
                    bA, bB = pend["A"], b
                    if "kout" in pend:
                        emit_kout(pend.pop("kout"))
                    if not isk:
                        vw = (lambda a: a) if g == 0 else (lambda a, g=g: a.rearrange("p (i r) -> p i r", r=GROUPS[g][1]))
                        rope_pair(bA, bB, n, qview(g, 0) if g else qT[:, 0, :], qview(g, 1) if g else qT[:, 1, :], rtmp,
                                  [qT.r[2 * g]], [qT.r[2 * g + 1]], view=vw)
                    else:
                        rope_pair(bA, bB, n, kf[:, 0, :], kf[:, 1, :], rtmp, [kf.r[0]], [kf.r[1]])
                        kres = [kT0.r, kT1.r, kT2.r][g]
                        for ab_ in range(2):
                            for dst, cols in k_dst(g, ab_):
                                src = kf[:, ab_, cols]
                                if g > 0:
                                    src = src.rearrange("p (i r) -> p i r", r=GROUPS[g][1])
                                act.op(lambda dst=dst, src=src: nc.scalar.copy(out=dst, in_=src),
                                       reads=[kf.r[ab_]], writes=kres)
                        pend["kout"] = g
                yield ("qk", cons_qk)

                mark("m.V")
                def v_evac(b, rows, ncols, dst_bf, dst_f32, res_bf):
                    act.op(lambda: nc.scalar.copy(out=dst_bf, in_=ps[b][rows, 0:ncols]), reads=[psr[b]],
                           writes=res_bf)
                    if dst_f32 is not None:
                        dve.op(lambda: nc.vector.tensor_copy(out=dst_f32, in_=ps[b][rows, 0:ncols]), reads=[psr[b]], writes=vtok.r)

                last = (t == NTILE - 1)
                for bp in range(2):
                    b = K.bank()
                    for bb in range(2):
                        b_ = 2 * bp + bb
                        K.mmg(b, [(ps[b][:, bb * 256:(bb + 1) * 256], hT[:, kc, b_ * 128:(b_ + 1) * 128], wv[:, kc, 0:256], {})
                                  for kc in range(8)], reads=hT.r + wv.r)
                    for bb in range(2):
                        b_ = 2 * bp + bb
                        slot = (4 * t + b_) % 5
                        act.op(lambda b=b, bb=bb, slot=slot: nc.scalar.copy(out=V0[:, slot, :], in_=ps[b][:, bb * 256:(bb + 1) * 256]), reads=[psr[b]], writes=V0.r)
                        if last and b_ == 3:
                            dve.op(lambda b=b, bb=bb: nc.vector.tensor_copy(out=vtok[:, 0, :], in_=ps[b][:, bb * 256:(bb + 1) * 256]),
                                   reads=[psr[b]], writes=vtok.r + [psr[b]])
                            st_dma('kv', kvp_d[0][s, :, 1, :], vtok[:, 0, :], vtok.r)
                hv4 = [hT4[:, kc, :].rearrange("p (r i) -> p r i", r=4) for kc in range(8)]
                for rp in range(2):
                    b = K.bank()
                    for rr in range(2):
                        r = 2 * rp + rr
                        K.mmg(b, [(ps[b][:, rr * 256:(rr + 1) * 256], hv4[kc][:, r, :], wv[:, kc, 256:512], {}) for kc in range(8)],
                              reads=hT4.r + wv.r)
                    for rr in range(2):
                        r = 2 * rp + rr
                        act.op(lambda b=b, rr=rr, r=r: nc.scalar.copy(out=V1[:, (t % 2) * 4 + r, :],
                                                                            in_=ps[b][:, rr * 256:(rr + 1) * 256]),
                               reads=[psr[b]], writes=V1.r)
                    for rr in range(2):
                        r = 2 * rp + rr
                        if last:
                            dve.op(lambda b=b, rr=rr, r=r: nc.vector.tensor_copy(out=vtok[:, r, :], in_=ps[b][:, rr * 256:(rr + 1) * 256]),
                                   reads=[psr[b]], writes=vtok.r + [psr[b]])
                if last:
                    st_dma('kv', kvp_d[1][s, :, 1, :].rearrange("(i r) f -> i r f", r=4), vtok[:, :, :], vtok.r)
                hv16 = [hT16[:, kc, :].rearrange("p (r i) -> p r i", r=16) for kc in range(8)]
                rows = slice(32 * t, 32 * t + 32)
                for rq in range(4):
                    vt_ = vtok if rq % 2 == 0 else vtok2
                    for rp in range(2):
                        b = K.bank()
                        for rr in range(2):
                            r = 4 * rq + 2 * rp + rr
                            K.mmg(b, [(ps[b][rows, rr * 256:(rr + 1) * 256], hv16[kc][:, r, :], wv[:, kc, 512:768], tp(0, 32 * t))
                                      for kc in range(8)], reads=hT16.r + wv.r)
                        ev = act if rp == 0 else dve
                        for rr in range(2):
                            r = 4 * rq + 2 * rp + rr
                            K.copy(ev, V2[rows, r, :], ps[b][rows, rr * 256:(rr + 1) * 256], [psr[b]], V2.r)
                            K.copy(ev, vt_[rows, r % 4, :], ps[b][rows, rr * 256:(rr + 1) * 256], [psr[b]], vt_.r)
                    dst = kvp_d[2][s, t * T:(t + 1) * T, 1, :].rearrange("(i r) f -> i r f", r=16)[:, 4 * rq:4 * rq + 4, :]
                    st_dma('kv' if rq % 2 == 0 else 'kv2', dst, vt_[rows, :, :], vt_.r)

                if "kout" in pend:
                    emit_kout(pend.pop("kout"))
                mark("m.pool")
                L = 15 + T
                for g in range(4):
                    src = uext[:, g, :]
                    cur = None
                    sh = 1
                    for step in range(g + 1):
                        dst = wtmp[step % 2]
                        srcap = src if cur is None else cur[:, :]
                        rd = [uext.r[g]] if cur is None else cur.r
                        pe_, pn_ = (dve, nc.vector) if K.pass_idx == 0 else (pool, nc.gpsimd)
                        pe_.op(lambda dst=dst, srcap=srcap, sh=sh, pn_=pn_: pn_.tensor_tensor(
                            out=dst[:, 2 * sh - 1:L], in0=srcap[:, 2 * sh - 1:L], in1=srcap[:, sh - 1:L - sh], op=ALU.add), reads=rd, writes=dst.r)
                        cur = dst
                        sh *= 2
                    w = 2 ** (g + 1)
                    if K.pass_idx == 0:
                        dve.op(lambda g=g, cur=cur, w=w: nc.vector.scalar_tensor_tensor(
                            out=pT[:, g, :], in0=cur[:, 15:L], scalar=1.0 / w, in1=uext[:, g, 15:L], op0=ALU.mult, op1=ALU.subtract),
                            reads=cur.r + [uext.r[g]], writes=[pT.r[g]])
                        if t == 0:
                            dve.op(lambda cur=cur, g=g: nc.vector.tensor_tensor(out=cur[:, 15:31], in0=cur[:, 15:31], in1=rcnt[:, g, :],
                                                                                op=ALU.mult), reads=cur.r + rcnt.r, writes=cur.r)
                            dve.op(lambda cur=cur, g=g: nc.vector.tensor_tensor(out=pT[:, g, 0:16], in0=cur[:, 15:31],
                                                                                in1=uext[:, g, 15:31], op=ALU.subtract),
                                   reads=cur.r + [uext.r[g]], writes=[pT.r[g]])
                    else:
                        fx = rden[:, g * 16:(g + 1) * 16]
                        if t == 0:
                            pool.op(lambda cur=cur, g=g, fx=fx: nc.gpsimd.tensor_tensor(out=fx, in0=cur[:, 15:31], in1=rcnt[:, g, :],
                                                                                        op=ALU.mult), reads=cur.r + rcnt.r, writes=rden.r)
                        pool.op(lambda cur=cur, w=w: nc.gpsimd.tensor_scalar(
                            cur[:, 15:L], cur[:, 15:L], 1.0 / w, 0.0, op0=ALU.mult, op1=ALU.add), reads=cur.r, writes=cur.r)
                        pool.op(lambda g=g, cur=cur: nc.gpsimd.tensor_tensor(out=pT[:, g, :], in0=cur[:, 15:L], in1=uext[:, g, 15:L],
                                                                             op=ALU.subtract), reads=cur.r + [uext.r[g]], writes=[pT.r[g]])
                        if t == 0:
                            pool.op(lambda g=g, fx=fx: nc.gpsimd.tensor_tensor(out=pT[:, g, 0:16], in0=fx, in1=uext[:, g, 15:31],
                                                                               op=ALU.subtract), reads=rden.r + [uext.r[g]], writes=[pT.r[g]])

                def pool_mm():
                    for g in range(4):
                        b = K.bank()
                        K.mmg(b, [(ps[b][:, 0:n], wgrp[:, g, :], pT[:, g, :], {})], reads=[pT.r[g]] + wgrp.r)
                        act.op(lambda g=g, b=b: nc.scalar.mul(poolT[:, g, :], ps[b][:, 0:n], pscale[:, g:g + 1]),
                               reads=[psr[b]] + pscale.r, writes=poolT.r)

                mark("m.attn")
                K.ps_free = [0, 1, 2, 3]
                numb = [4, 5]
                denb = [6, 7]
                for b in numb + denb:
                    K.mmg(b, [(ps[b][:, :], zeros[0:1, 0:128], zeros[0:1, 0:512], {"start": True, "stop": False, "skip_group_check": True})],
                          reads=zeros.r)
                pi = [0]

                def softmax_block(b, krows, c0, ncols, mask_ap, mask_rows, mres):
                    pb = pbuf[pi[0] % 3]
                    pi[0] += 1
                    act.op(lambda: nc.scalar.activation(out=pb[0:krows, c0:c0 + ncols], in_=ps[b][0:krows, c0:c0 + ncols], func=AF.Exp,
                                                        scale=0.125), reads=[psr[b]], writes=pb.r)
                    if mask_ap is not None:
                        dve.op(lambda: nc.vector.tensor_tensor(out=pb[mask_rows, c0:c0 + ncols], in0=pb[mask_rows, c0:c0 + ncols],
                                                               in1=mask_ap, op=ALU.mult), reads=pb.r + mres, writes=pb.r)
                    return pb

                def unit01(g, h, which, units):
                    kTg, Vg = (kT0, V0) if g == 0 else (kT1, V1)
                    c, half = h // 2, (h % 2) * 64
                    b = K.bank()
                    items = []
                    for (u_, slot) in units:
                        for ab in range(2):
                            items.append((ps[b][:, u_ * 128:(u_ + 1) * 128], kTg[32 * h:32 * h + 32, ab, slot * 128:(slot + 1) * 128],
                                          qT[32 * h:32 * h + 32, 2 * g + ab, u_ * 128:(u_ + 1) * 128],
                                          dict(start=(ab == 0), stop=(ab == 1), **tp(32 * h, 0))))
                    K.mmg(b, items, reads=kTg.r + [qT.r[2 * g], qT.r[2 * g + 1]])
                    c0 = units[0][0] * 128
                    ncols = len(units) * 128
                    mi = 0 if which == "prev" else 1
                    pb = softmax_block(b, 128, c0, ncols, mask01[:, mi, 0:len(units), :].rearrange("p h i -> p (h i)"),
                                       slice(0, 128), mask01.r)
                    yield
                    nu = len(units)
                    for ui, (u_, slot) in enumerate(units):
                        for isden in (False, True):
                            bb = denb[c] if isden else numb[c]
                            lhsT = ones[:, 0:64] if isden else Vg[:, slot, h * 64:(h + 1) * 64]
                            if g == 0:
                                outap = ps[bb][half:half + 64, u_ * 128:(u_ + 1) * 128]
                            else:
                                outap = ps[bb][half:half + 64, :].rearrange("p (i r) -> p r i", r=4)[:, u_, :]
                            first = (ui == 0 and not isden)
                            lastm = (ui == nu - 1 and isden)
                            pe.op(lambda outap=outap, lhsT=lhsT, pb=pb, u_=u_: nc.tensor.matmul(
                                outap, lhsT, pb[:, u_ * 128:(u_ + 1) * 128], start=False, stop=False, skip_group_check=True),
                                reads=Vg.r + pb.r + ones.r if first else (), writes=(), inc=lastm)
                    tok = (pe.sem, pe.cnt)
                    EngW.commit(tok, Vg.r + pb.r, [psr[x_] for x_ in numb + denb])

                kr = 32 * (t + 1)

                def unit2(h):
                    b = K.bank()
                    items = []
                    for r in range(16):
                        for ab in range(2):
                            items.append((ps[b][0:kr, r * 32:(r + 1) * 32], kT2[32 * h:32 * h + 32, ab, r * 128:r * 128 + kr],
                                          qT[32 * h:32 * h + 32, 4 + ab, r * 32:(r + 1) * 32],
                                          dict(start=(ab == 0), stop=(ab == 1), **tp(32 * h, 0))))
                    K.mmg(b, items, reads=kT2.r + [qT.r[4], qT.r[5]])
                    pb = softmax_block(b, kr, 0, 512, mask2[32 * t:32 * t + 32, :, :].rearrange("p r i -> p (r i)"),
                                       slice(32 * t, 32 * t + 32), mask2.r)
                    yield
                    c, half = h // 2, (h % 2) * 64
                    for r in range(16):
                        for bank_list, isden in ((numb, False), (denb, True)):
                            bb = bank_list[c]
                            lhsT = ones[0:kr, 0:64] if isden else V2[0:kr, r, h * 64:(h + 1) * 64]
                            outap = ps[bb][half:half + 64, :].rearrange("p (i r) -> p r i", r=16)[:, r, :]
                            pe.op(lambda outap=outap, lhsT=lhsT, pb=pb, r=r, h=h: nc.tensor.matmul(
                                outap, lhsT, pb[0:kr, r * 32:(r + 1) * 32], start=False, stop=False, skip_group_check=True),
                                reads=V2.r + pb.r + ones.r if (r == 0 and not isden) else (), writes=(), inc=(r == 15 and isden))
                    tok = (pe.sem, pe.cnt)
                    EngW.commit(tok, V2.r + pb.r, [psr[x_] for x_ in numb + denb])

                makers = []
                for g in (0, 1):
                    for h in range(4):
                        for which in ("cur", "prev"):
                            units = []
                            for u_ in range(4):
                                if g == 0:
                                    babs = 4 * t + u_
                                    if which == "cur":
                                        units.append((u_, babs % 5))
                                    elif babs > 0:
                                        units.append((u_, (babs - 1) % 5))
                                else:
                                    if which == "cur":
                                        units.append((u_, (t % 2) * 4 + u_))
                                    elif t > 0:
                                        units.append((u_, ((t - 1) % 2) * 4 + u_))
                            if units:
                                makers.append(lambda g=g, h=h, which=which, units=units: unit01(g, h, which, units))
                for h in range(4):
                    makers.append(lambda h=h: unit2(h))
                DEPTH = 2
                live = []
                for mk in makers:
                    gen = mk()
                    next(gen)
                    live.append(gen)
                    if len(live) > DEPTH:
                        for _ in live.pop(0):
                            pass
                for gen in live:
                    for _ in gen:
                        pass
                for c in range(2):
                    rd_ = gt[c]
                    act.op(lambda c=c, rd_=rd_: nc.scalar.activation(out=rd_[:, :], in_=ps[denb[c]][:, :], func=AF.Ln), reads=[psr[denb[c]]],
                           writes=rd_.r)
                    act.op(lambda rd_=rd_: nc.scalar.activation(out=rd_[:, :], in_=rd_[:, :], func=AF.Exp, scale=-1.0), reads=rd_.r, writes=rd_.r)
                    dve.op(lambda c=c, rd_=rd_: nc.vector.tensor_tensor(out=attnT[:, c, :], in0=ps[numb[c]][:, :], in1=rd_[:, :], op=ALU.mult),
                           reads=[psr[numb[c]]] + rd_.r, writes=attnT.r)
                K.ps_free = list(range(8))

                pool_mm()
                mark("m.gates")
                yield ("gates", (SP, poolT, attnT, mergedT, gt))

        K.final_sems = [st_sems[i][k] for i in range(2) for k in ("kv", "kv2", "kt", "pt", "y", "ys")]

        def mixer_sample(shared):
            n = NS
            hT = hTs
            norm(1, SS)
            with arena():
                unew = Buf(K, "unew", [128, 4, NS], F32)
                uexs = Buf(K, "uexs", [128, 4, NS, 16], F32)
                al = bool(shared)
                sptok = shared["ptok"] if al else Buf(K, "sptok", [16, 512], F32)
                wsum = Buf(K, "wsum", [128, 4, NS], F32)
                pT = Buf(K, "pTs", [128, 4, NS], BF16)
                poolT = Buf(K, "poolTs", [128, 4, NS], BF16)
                qT = Buf(K, "qTs", [128, 6, NS], BF16)
                rtmp = [Buf(K, "rtmps%d" % i, [128, NS], F32) for i in range(4)]
                big = shared["rtmp"] if al else [Buf(K, "bigs%d" % i, [128, 512], F32) for i in range(3)]
                kf = Buf(K, "kfs", [128, 6, NS], F32)
                kTs = Buf(K, "kTs", [128, 6, NS], BF16)
                ktok = shared["ktok"] if al else Buf(K, "ktoks", [128, 4, 256], F32)
                if al:
                    vt_a, vt_b = shared["vtok"], shared["vtok2"]
                    utok = View(vt_a, lambda: vt_a[0:NS, 0:2, :].rearrange("p a c -> p (a c)"))
                    vrow_f = View(vt_b, lambda: vt_b[0:1, 0:3, :].rearrange("p a c -> p (a c)"))
                else:
                    utok = Buf(K, "utoks", [NS, 512], F32)
                    vrow_f = Buf(K, "vrowf", [1, 768], F32)
                vrow_bb = shared["hT16"] if al else Buf(K, "vrowb", [128, 8, T], BF16)
                vrow_b_ap = lambda bb, c0, c1: vrow_bb[0:1, 0:6, :].rearrange("p a c -> p (a c)")[:, bb * 768 + c0:bb * 768 + c1]
                kctok = big[0:2]
                kcperm = big[2]
                kcT = Buf(K, "kcT", [128, 2, 128], BF16)
                vcbb = shared["mergedT"] if al else Buf(K, "vcb", [128, 8, T], BF16)
                vcb_ap = lambda idx, c0=0, c1=256: vcbb[:, 0:6, :].rearrange("p a (b c) -> p (a b) c", c=256)[:, idx, c0:c1]
                pTh = [Buf(K, "pTh%d" % h, [128, 12], BF16) for h in range(4)]
                pself = [Buf(K, "pself%d" % h, [1, 12], BF16) for h in range(4)]
                attnT = Buf(K, "attnTs", [128, 2, NS], BF16)
                rden = Buf(K, "rdens", [128, 8], F32)
                mergedT = Buf(K, "mergedTs", [128, 8, NS], BF16, nres=8)
                gt = [Buf(K, "gts%d" % i, [128, NS], F32) for i in range(2)]
                wv = shared["wv"] if al else Buf(K, "wvs", [128, 8, 768], BF16)
                ds_kc = [K.dsem("ds_kc%d" % i) for i in range(2)]
                K.arena_dsems.extend(ds_kc)

                for tk in K.arena_toks:
                    sp.wait_tok(tk)
                if not al:
                    wload(K.pass_idx == 0, wv, wv[:], win_d.rearrange("(kc p) n -> p kc n", p=128)[:, :, OFF_V:OFF_V + 768],
                          wscrv.rearrange("p (k n) -> p k n", n=768), scrv_res, wv_sem3, True)
                for bb in range(NS):
                    sp.dma(sptok[0:15, :], spool_d[bb, :, :], ds_sp, writes=sptok.r)
                    b = K.bank()
                    K.mmg(b, [(ps[b][:, g * 16:g * 16 + 15], sptok[0:15, g * 128:(g + 1) * 128], ident[0:15, 0:15],
                               {"start": True, "stop": True, "is_transpose": True}) for g in range(4)], reads=sptok.r + ident.r)
                    act.op(lambda b=b, bb=bb: nc.scalar.copy(out=uexs[:, :, bb, 0:15],
                                                             in_=ps[b][:, 0:64].rearrange("p (g r) -> p g r", r=16)[:, :, 0:15]),
                           reads=[psr[b]], writes=uexs.r)

                def cons_u(g, b):
                    act.op(lambda: nc.scalar.copy(out=unew[:, g, :], in_=ps[b][:, 0:n]), reads=[psr[b]], writes=unew.r)
                yield ("u", cons_u)
                dve.op(lambda: nc.vector.tensor_copy(out=uexs[:, :, :, 15], in_=unew[:, :, :]), reads=unew.r, writes=uexs.r)
                b = K.bank()
                K.mmg(b, [(ps[b][0:NS, g * 128:(g + 1) * 128], unew[:, g, :], ident[:], {"start": True, "stop": True, "is_transpose": True})
                          for g in range(4)], reads=unew.r + ident.r)
                act.op(lambda: nc.scalar.copy(out=utok[:, :], in_=ps[b][0:NS, :]), reads=[psr[b]], writes=utok.r)
                st_dma('pt', pools_d[:, 14, :], utok[:, :], utok.r)
                for g in range(4):
                    w = 2 ** (g + 1)
                    dve.op(lambda g=g, w=w: nc.vector.tensor_reduce(out=wsum[:, g, :], in_=uexs[:, g, :, 16 - w:16],
                                                                    axis=mybir.AxisListType.X, op=ALU.add),
                           reads=uexs.r, writes=wsum.r)
                    dve.op(lambda g=g, w=w: nc.vector.scalar_tensor_tensor(out=pT[:, g, :], in0=wsum[:, g, :], scalar=1.0 / w,
                                                                           in1=unew[:, g, :], op0=ALU.mult, op1=ALU.subtract),
                           reads=wsum.r + unew.r, writes=pT.r)
                    b = K.bank()
                    K.mmg(b, [(ps[b][:, 0:n], wgrp[:, g, :], pT[:, g, :], {})], reads=pT.r + wgrp.r)
                    act.op(lambda g=g, b=b: nc.scalar.mul(poolT[:, g, :], ps[b][:, 0:n], pscale[:, g:g + 1]),
                           reads=[psr[b]] + pscale.r, writes=poolT.r)
                pend = {}

                def cons_qk(idx, b):
                    isk = idx >= 6
                    g = (idx % 6) // 2
                    ab = idx % 2
                    if ab == 0:
                        pend["A"] = b
                        return
                    bA, bB = pend["A"], b
                    if not isk:
                        rope_pair(bA, bB, n, qT[:, 2 * g, :], qT[:, 2 * g + 1, :], rtmp, qT.r, qT.r, rope=rope_s)
                    else:
                        rope_pair(bA, bB, n, kf[:, 2 * g, :], kf[:, 2 * g + 1, :], rtmp, kf.r, kf.r, rope=rope_s)
                yield ("qk", cons_qk)
                act.op(lambda: nc.scalar.copy(out=kTs[:, :, :], in_=kf[:, :, :]), reads=kf.r, writes=kTs.r)
                b = K.bank()
                K.mmg(b, [(ps[b][0:NS, cc * 128:(cc + 1) * 128], kf[:, cc, :], ident[:], {"start": True, "stop": True, "is_transpose": True})
                          for cc in range(4)], reads=kf.r + ident.r)
                b2 = K.bank()
                K.mmg(b2, [(ps[b2][0:NS, (cc - 4) * 128:(cc - 3) * 128], kf[:, cc, :], ident[:], {"start": True, "stop": True, "is_transpose": True})
                           for cc in range(4, 6)], reads=kf.r + ident.r)
                for g in range(3):
                    bk, c0 = (b, g * 256) if g < 2 else (b2, 0)
                    act.op(lambda g=g, bk=bk, c0=c0: nc.scalar.copy(
                        out=ktok[0:NS, g, :].rearrange("p (h ab i) -> p ab h i", ab=2, i=32),
                        in_=ps[bk][0:NS, c0:c0 + 256].rearrange("p (ab h i) -> p ab h i", ab=2, i=32)),
                        reads=[psr[bk]], writes=ktok.r)
                for g in range(3):
                    Lg = GROUPS[g][0]
                    st_dma('kt', kvs_d[g][:, Lg - 1, 0, :], ktok[0:NS, g, :], ktok.r)
                for bb in range(NS):
                    b = K.bank()
                    K.mmg(b, [(ps[b][0:1, 0:512], hT[:, kc, bb:bb + 1], wv[:, kc, 0:512], {}) for kc in range(8)], reads=hT.r + wv.r)
                    b2 = K.bank()
                    K.mmg(b2, [(ps[b2][0:1, 0:256], hT[:, kc, bb:bb + 1], wv[:, kc, 512:768], {}) for kc in range(8)], reads=hT.r + wv.r)
                    act.op(lambda b=b, bb=bb: nc.scalar.copy(out=vrow_f[0:1, 0:512], in_=ps[b][0:1, 0:512]), reads=[psr[b]], writes=vrow_f.r)
                    act.op(lambda b2=b2, bb=bb: nc.scalar.copy(out=vrow_f[0:1, 512:768], in_=ps[b2][0:1, 0:256]), reads=[psr[b2]],
                           writes=vrow_f.r)
                    dve.op(lambda bb=bb: nc.vector.tensor_copy(out=vrow_b_ap(bb, 0, 768), in_=vrow_f[0:1, :]), reads=vrow_f.r,
                           writes=vrow_bb.r)
                    for g in range(3):
                        Lg = GROUPS[g][0]
                        st_dma('kv', kvs_d[g][bb, Lg - 1, 1:2, :], vrow_f[0:1, g * 256:(g + 1) * 256], vrow_f.r)
                K.ps_free = [0, 1, 2]
                hb = [3, 4, 5, 6]
                accb = 7
                K.mmg(accb, [(ps[accb][:, :], zeros[0:1, 0:128], zeros[0:1, 0:512], {"start": True, "stop": False, "skip_group_check": True})],
                      reads=zeros.r)
                ci = 0
                for bb in range(NS):
                    for g in range(3):
                        Lg, dil = GROUPS[g]
                        idx = bb * 3 + g
                        kc_ = kctok[ci % 2]
                        dsm = ds_kc[ci % 2]
                        ci += 1
                        sp.dma(kc_[:, :].rearrange("p (t f) -> p t f", t=2), cache_d[g][bb, 0:Lg:dil, :, :], dsm, writes=kc_.r)
                        dve.op(lambda kc_=kc_: nc.vector.tensor_copy(out=kcperm[:, 0:256].rearrange("p (ab h i) -> p ab h i", ab=2, i=32),
                                                                     in_=kc_[:, 0:256].rearrange("p (h ab i) -> p ab h i", ab=2, i=32)),
                               reads=kc_.r, writes=kcperm.r)
                        act.op(lambda kc_=kc_, idx=idx: nc.scalar.copy(out=vcb_ap(idx), in_=kc_[:, 256:512]), reads=kc_.r, writes=vcbb.r)
                        b = K.bank()
                        K.mmg(b, [(ps[b][:, ab * 128:(ab + 1) * 128], kcperm[:, ab * 128:(ab + 1) * 128], ident[:],
                                   {"start": True, "stop": True, "is_transpose": True}) for ab in range(2)], reads=kcperm.r + ident.r)
                        dve.op(lambda b=b: nc.vector.tensor_copy(out=kcT[:, :, :].rearrange("p a k -> p (a k)"), in_=ps[b][:, 0:256]),
                               reads=[psr[b]], writes=kcT.r)
                        for h in range(4):
                            K.mmg(hb[h], [(ps[hb[h]][:, idx:idx + 1], kcT[32 * h:32 * h + 32, ab, :], qT[32 * h:32 * h + 32, 2 * g + ab, bb:bb + 1],
                                           dict(start=(ab == 0), stop=(ab == 1), **tp(32 * h, 0))) for ab in range(2)],
                                  reads=kcT.r + qT.r)
                            K.mmg(hb[h], [(ps[hb[h]][0:1, 16 + idx:17 + idx], kTs[32 * h:32 * h + 32, 2 * g + ab, bb:bb + 1],
                                           qT[32 * h:32 * h + 32, 2 * g + ab, bb:bb + 1],
                                           dict(start=(ab == 0), stop=(ab == 1), **tp(32 * h, 0))) for ab in range(2)],
                                  reads=kTs.r + qT.r)
                for h in range(4):
                    act.op(lambda h=h: nc.scalar.activation(out=pTh[h][:, :], in_=ps[hb[h]][:, 0:12], func=AF.Exp, scale=0.125),
                           reads=[psr[hb[h]]], writes=pTh[h].r)
                    act.op(lambda h=h: nc.scalar.activation(out=pself[h][0:1, :], in_=ps[hb[h]][0:1, 16:28], func=AF.Exp, scale=0.125),
                           reads=[psr[hb[h]]], writes=pself[h].r)
                for h in range(4):
                    c, half = h // 2, (h % 2) * 64
                    n_mm = 0
                    for bb in range(NS):
                        for g in range(3):
                            idx = bb * 3 + g
                            for isden in (False, True):
                                col = (2 * isden + c) * 4 + bb
                                outap = ps[accb][half:half + 64, col:col + 1]
                                l1 = ones[:, 0:64] if isden else vcb_ap(idx, h * 64, (h + 1) * 64)
                                l2 = ones[0:1, 0:64] if isden else vrow_b_ap(bb, g * 256 + h * 64, g * 256 + (h + 1) * 64)
                                first = (n_mm == 0)
                                pe.op(lambda outap=outap, l1=l1, h=h, idx=idx: nc.tensor.matmul(
                                    outap, l1, pTh[h][:, idx:idx + 1], start=False, stop=False, skip_group_check=True),
                                    reads=vcbb.r + vrow_bb.r + pTh[h].r + pself[h].r + ones.r if first else (), writes=(), inc=False)
                                lastm = (bb == NS - 1 and g == 2 and isden)
                                pe.op(lambda outap=outap, l2=l2, h=h, idx=idx: nc.tensor.matmul(
                                    outap, l2, pself[h][0:1, idx:idx + 1], start=False, stop=False, skip_group_check=True),
                                    reads=(), writes=(), inc=lastm)
                                n_mm += 1
                    tok = (pe.sem, pe.cnt)
                    EngW.commit(tok, vcbb.r + vrow_bb.r + pTh[h].r + pself[h].r, [psr[accb]])
                dve.op(lambda: nc.vector.reciprocal(out=rden[:, :], in_=ps[accb][:, 8:16]), reads=[psr[accb]], writes=rden.r)
                for c in range(2):
                    dve.op(lambda c=c: nc.vector.tensor_tensor(out=attnT[:, c, :], in0=ps[accb][:, c * 4:(c + 1) * 4], in1=rden[:, c * 4:(c + 1) * 4],
                                                               op=ALU.mult), reads=[psr[accb]] + rden.r, writes=attnT.r)
                K.ps_free = list(range(8))
                yield ("gates", (SS, poolT, attnT, mergedT, gt))
                for d in ds_kc:
                    K.arena_dsems.remove(d)

        def run_mixer(s_, t_, segs):
            shared = {}
            gens = []
            if SP in segs:
                gens.append(mixer_prompt(s_, t_, shared))
            if SS in segs:
                gens.append(mixer_sample(shared))
            r = [next(g) for g in gens]
            assert all(x[0] == "u" for x in r)
            proj_chunks(4, segs, [x[1] for x in r])
            r = [next(g) for g in gens]
            assert all(x[0] == "qk" for x in r)
            proj_chunks(12, segs, [x[1] for x in r])
            r = [next(g) for g in gens]
            assert all(x[0] == "gates" for x in r)
            gates_and_wo([x[1] for x in r])
            for g in gens[::-1]:
                for _ in g:
                    raise AssertionError("unexpected yield")

        def sample_in():
            sp.dma(xin[0:NS, 0, :], xs_d[:, :], ds_x, writes=xin.r)
            sp.dma(rope_s[:], rope_d[4].rearrange("a p n -> p a n")[:, :, 0:NS], ds_rope, writes=rope_s.r)
            b = K.bank()
            K.mmg(b, [(ps[b][:, c * 4:(c + 1) * 4], xin[0:NS, 0, c * 128:(c + 1) * 128], ident[0:NS, 0:NS],
                       {"start": True, "stop": True, "is_transpose": True}) for c in range(8)], reads=xin.r + ident.r)
            dve.op(lambda: nc.vector.tensor_copy(out=xTs[:, :, 0:NS], in_=ps[b][:, 0:32].rearrange("p (c n) -> p c n", n=NS)),
                   reads=[psr[b]], writes=xTs.r)

        K.pending_copies = []

        def sample_copies():
            for g in (2, 1, 0):
                Lg = GROUPS[g][0]
                for bb in range(NS):
                    nel = (Lg - 1) * 512
                    a = 16 if nel % 16 == 0 else 1
                    src = cache_d[g][bb, 1:Lg, :, :].rearrange("r t f -> (r t f)").rearrange("(a c) -> a c", a=a)
                    dst = kvs_d[g][bb, 0:Lg - 1, :, :].rearrange("r t f -> (r t f)").rearrange("(a c) -> a c", a=a)
                    K.pending_copies.append((dst, src))
            K.pending_copies.append((pools_d[:, 0:14, :], spool_d[:, 1:15, :]))

        def issue_copy(nmax=1):
            for _ in range(nmax):
                if K.pending_copies:
                    dst, src = K.pending_copies.pop(0)
                    (sp if K.pass_idx == 0 else pool).dma(dst, src, ds_misc)

        def sample_out():
            norm(3, SS, out_f32=True)
            ystg = rope
            b = K.bank()
            K.mmg(b, [(ps[b][0:NS, cc * 128:(cc + 1) * 128], xTs[:, cc, 0:NS], ident[:], {"start": True, "stop": True, "is_transpose": True})
                      for cc in range(4)], reads=xTs.r + ident.r)
            b2 = K.bank()
            K.mmg(b2, [(ps[b2][0:NS, (cc - 4) * 128:(cc - 3) * 128], xTs[:, cc, 0:NS], ident[:], {"start": True, "stop": True, "is_transpose": True})
                       for cc in range(4, 8)], reads=xTs.r + ident.r)
            act.op(lambda: nc.scalar.copy(out=ystg[0:NS, 0, :], in_=ps[b][0:NS, :]), reads=[psr[b]], writes=ystg.r)
            act.op(lambda: nc.scalar.copy(out=ystg[0:NS, 1, :], in_=ps[b2][0:NS, :]), reads=[psr[b2]], writes=ystg.r)
            st_dma('ys', ys_d[:, :], ystg[0:NS, :, :].rearrange("p a n -> p (a n)"), ystg.r)

        K.x_prefetched = False
        for s in range(nseq):
            for t in range(ntile):
                n = T
                K.pass_idx = s * ntile + t
                with_s = do_sample and s == 0 and t == 0
                segs = [SP, SS] if with_s else [SP]
                mark("load")
                if not K.x_prefetched:
                    sp.dma(xin[:, :, :], x_d[s, t * T:(t + 1) * T, :].rearrange("(b p) d -> p b d", p=128), ds_x, writes=xin.r)
                K.x_prefetched = False
                for c in range(8):
                    b = K.bank()
                    K.mmg(b, [(ps[b][:, b_ * 128:(b_ + 1) * 128], xin[:, b_, c * 128:(c + 1) * 128], ident[:],
                               {"start": True, "stop": True, "is_transpose": True}) for b_ in range(4)], reads=xin.r + ident.r)
                    K.copy(K.evac_eng(), xT[:, c, :], ps[b][:, :], [psr[b]], [xT.r[c]])
                if with_s:
                    sample_in()
                if K.pass_idx == 1 or (with_s and K.pass_idx == 0):
                    pass
                mark("ffn1")
                ffn_stage(0, 0, segs)
                mark("mixer")
                run_mixer(s, t, segs)
                mark("ffn2")
                nxt = s * ntile + t + 1
                if nxt < nseq * ntile:
                    s2, t2 = divmod(nxt, ntile)
                    sp.dma(xin[:, :, :], x_d[s2, t2 * T:(t2 + 1) * T, :].rearrange("(b p) d -> p b d", p=128), ds_x, writes=xin.r)
                    K.x_prefetched = True

                def tail(ystage, s=s, t=t):
                    mark("final")
                    norm(3, SP, out_f32=True)
                    for b_ in range(4):
                        for cq in range(2):
                            b = K.bank()
                            K.mmg(b, [(ps[b][:, cc * 128:(cc + 1) * 128], xT[:, 4 * cq + cc, b_ * 128:(b_ + 1) * 128], ident[:],
                                       {"start": True, "stop": True, "is_transpose": True}) for cc in range(4)],
                                  reads=[xT.r[4 * cq + cc] for cc in range(4)] + ident.r)
                            K.copy(K.evac_eng(), ystage[:, b_, cq * 512:(cq + 1) * 512], ps[b][:, :], [psr[b]], ystage.r)
                    return st_dma('y', y_d[s, t * T:(t + 1) * T, :].rearrange("(b p) d -> p b d", p=128), ystage[:, :, :], ystage.r)
                ffn_stage(1, 2, segs, tail=tail)
                if with_s:
                    sample_out()
                if do_sample and K.pass_idx == min(1, nseq * ntile - 1):
                    sample_copies()

        mark("sample")
        if do_sample and nseq * ntile == 0:
            K.pass_idx = 0
            sample_in()
            sample_copies()
            ffn_stage(0, 0, [SS])
            run_mixer(0, 0, [SS])
            ffn_stage(1, 2, [SS])
            sample_out()

        issue_copy(100)
        for d in [ds_misc] + K.final_sems:
            if d.total > 0:
                sp.wait_tok((d.sem, d.total))
    nc._marks = K.marks + [("end", getattr(K.pe, "nops", 0))]
    return nc


def _win_perm():
    cols = list(range(512))
    for which in (0, 1):
        for g in range(3):
            for half in (0, 1):
                for h in range(4):
                    base = 512 + g * 768 + which * 256 + h * 64 + half * 32
                    cols.extend(range(base, base + 32))
    for g in range(3):
        base = 512 + g * 768 + 2 * 256
        cols.extend(range(base, base + 256))
    cols.extend(range(2816, 4864))
    return np.asarray(cols, dtype=np.int64)


def _consts():
    ident = np.eye(128, dtype=np.float32)
    kk = np.arange(128)[:, None]
    ii = np.arange(128)[None, :]
    m_prev = (kk >= ii).astype(np.float32)
    m_cur = (kk <= ii).astype(np.float32)
    mask01 = np.stack([np.broadcast_to(m_prev[:, None, :], (128, 4, 128)), np.broadcast_to(m_cur[:, None, :], (128, 4, 128))], axis=1)
    m2 = ((np.arange(128)[:, None] % 32) <= np.arange(32)[None, :]).astype(np.float32)
    mask2 = np.broadcast_to(m2[:, None, :], (128, 16, 32))
    half = 32
    inv = (np.float32(10000.0) ** (-(np.arange(half, dtype=np.float32)) * np.float32(2.0 / 64))).astype(np.float32)
    rope = np.zeros((5, 2, 128, 512), np.float32)
    for t in range(5):
        pos = (np.arange(512) + 512 * t).astype(np.float32) if t < 4 else np.full(512, PAST, np.float32)
        ang = (pos[None, :] * inv[:, None]).astype(np.float32)
        rope[t, 0] = np.tile(np.cos(ang).astype(np.float32), (4, 1))
        rope[t, 1] = np.tile(np.sin(ang).astype(np.float32), (4, 1))
    rcnt = np.zeros((128, 4, 16), np.float32)
    for g, w in enumerate((2, 4, 8, 16)):
        rcnt[:, g, :] = (np.float32(1.0) / np.minimum(np.arange(16) + 1, w).astype(np.float32))[None, :]
    return dict(ident=ident, mask01=np.ascontiguousarray(mask01), mask2=np.ascontiguousarray(mask2), rope=rope, rcnt=rcnt)


def make_in_maps(inp):
    perm = _win_perm()
    c = _consts()
    f = lambda a: np.ascontiguousarray(np.asarray(a, dtype=np.float32))
    shared = dict(
        gains=f(np.stack([inp["norm_ffn1"][0], inp["norm_mix"][0], inp["norm_ffn2"][0], inp["norm_final"]])),
        pscale=f(inp["pool_scale"][0]),
        wg1=f(inp["ffn1_w_gate"][0]), wu1=f(inp["ffn1_w_up"][0]), wd1=f(inp["ffn1_w_down"][0]),
        wg2=f(inp["ffn2_w_gate"][0]), wu2=f(inp["ffn2_w_up"][0]), wd2=f(inp["ffn2_w_down"][0]),
        win=f(np.asarray(inp["w_in"][0])[:, perm]), wgrp=f(inp["w_pool_grp"][0]), wpb=f(inp["w_pool_br"][0]),
        wab=f(inp["w_attn_br"][0]), wo=f(inp["w_o"][0]), **c)
    maps = []
    for k in range(8):
        m = dict(shared)
        m["x"] = f(inp["x_prompt"][NSEQ * k:NSEQ * (k + 1)])
        m["xs"] = f(np.asarray(inp["x_sample"])[NS * k:NS * (k + 1), 0, :])
        m["spool"] = f(np.asarray(inp["state_pool"])[0, NS * k:NS * (k + 1)])
        for (w, _), nm in zip(GROUPS, ("cache_kv_w128", "cache_kv_w512", "cache_kv_w2048")):
            m["c%d" % w] = f(np.asarray(inp[nm])[0, NS * k:NS * (k + 1)].reshape(NS, w, 2, 256))
        maps.append(m)
    return maps


def assemble(results):
    cat = lambda nm: np.concatenate([r[nm] for r in results], axis=0)
    y = cat("y")
    ys = cat("ys")[:, None, :]
    poolp = cat("poolp")[None]
    kvp = [cat("kvp%d" % w).reshape(16, min(w, SEQ), 2, 4, 64)[None] for (w, _) in GROUPS]
    pools = cat("pools")[None]
    kvs = [cat("kvs%d" % w).reshape(32, w, 2, 4, 64)[None] for (w, _) in GROUPS]
    return (y, ys, poolp, kvp[0], kvp[1], kvp[2], pools, kvs[0], kvs[1], kvs[2])


def kernel(**inputs):
    nc = build(CFG)
    in_maps = make_in_maps(inputs)
    res = run_bass_kernel_spmd(nc, in_maps, core_ids=list(range(8)))
    return tuple(np.ascontiguousarray(a, dtype=np.float32) for a in assemble(res.results))
```
